# Optimizing a Trainium2 kernel written in Bass

```python
import math
import jax, jax.numpy as jnp
from jax import lax
import numpy as np

D_MODEL = 1024
BATCH = 16
SEQ = 2048
DEPTH = 1
DEC_BATCH = 32
DEC_SEQ = 32
PAST_LEN = 2048

CHUNK = 64
Q_BLOCK = 128
N_HEADS_A = D_MODEL // 256
HEAD_DIM_A = 64
V_DIM_A = 2 * HEAD_DIM_A
QK_WIDTH = N_HEADS_A * 2 * HEAD_DIM_A
WIDTH_A = N_HEADS_A * V_DIM_A
WIDTH_B = D_MODEL - WIDTH_A
A_COLS = 2 * QK_WIDTH + WIDTH_A
IN_COLS = A_COLS + 2 * WIDTH_B
CONV_WIDTH = 31
CONV_PAD = CONV_WIDTH - 1
D_FF = 4 * D_MODEL
N_BUCKETS = 32
MAX_DISTANCE = 128
EPS = 1e-6
NEG_INF = -1e30

kernel_name = 'hymba_diffattn_conformer_stream_step'


def _rms_norm(x, g):
    xf = x.astype(jnp.float32)
    y = xf * lax.rsqrt(jnp.mean(xf * xf, axis=-1, keepdims=True) + EPS)
    return (y * g.astype(jnp.float32)).astype(x.dtype)


def _layer_norm(x, g, b):
    xf = x.astype(jnp.float32)
    mu = jnp.mean(xf, axis=-1, keepdims=True)
    var = jnp.mean(jnp.square(xf - mu), axis=-1, keepdims=True)
    y = (xf - mu) * lax.rsqrt(var + EPS)
    return (y * g.astype(jnp.float32) + b.astype(jnp.float32)).astype(x.dtype)


def _rel_bucket(rel):
    half = N_BUCKETS // 2
    max_exact = half // 2
    n = -rel
    ret = jnp.where(n < 0, half, 0)
    n = jnp.abs(n)
    nf = jnp.maximum(n, 1).astype(jnp.float32)
    large = max_exact + (jnp.log(nf / max_exact) / math.log(MAX_DISTANCE / max_exact)
                         * (half - max_exact)).astype(jnp.int32)
    large = jnp.minimum(large, half - 1)
    return ret + jnp.where(n < max_exact, n, large)


def _diff_attention(q, k, v, past_len, rel_bias, lam, subln_g, lam_init):
    B, T = q.shape[0], q.shape[1]
    n_keys = k.shape[1]
    scale = HEAD_DIM_A ** -0.5
    qh = q.reshape(B, T, N_HEADS_A, 2, HEAD_DIM_A) * scale
    kh = k.reshape(B, n_keys, N_HEADS_A, 2, HEAD_DIM_A)
    outs = []
    for start in range(0, T, Q_BLOCK):
        stop = min(start + Q_BLOCK, T)
        k_end = past_len + stop
        q_pos = past_len + jnp.arange(start, stop)
        k_pos = jnp.arange(k_end)
        bias = rel_bias[_rel_bucket(k_pos[None, :] - q_pos[:, None])]
        bias = jnp.transpose(bias, (2, 0, 1)).astype(jnp.float32)
        mask = (k_pos[None, :] // CHUNK) <= (q_pos[:, None] // CHUNK)
        s = jnp.einsum('bqhmd,bkhmd->bhmqk', qh[:, start:stop], kh[:, :k_end]).astype(jnp.float32)
        s = jnp.where(mask, s + bias[None, :, None], NEG_INF)
        p = jax.nn.softmax(s, axis=-1)
        attn = p[:, :, 0] - lam * p[:, :, 1]
        outs.append(jnp.einsum('bhqk,bkhd->bqhd', attn.astype(v.dtype), v[:, :k_end]))
    o = jnp.concatenate(outs, axis=1)
    o = _rms_norm(o, subln_g) * (1.0 - lam_init)
    return o.reshape(B, T, WIDTH_A)


def _causal_depthwise_conv(u_full, w_dw, b_dw):
    y = lax.conv_general_dilated(u_full, w_dw[:, None, :].astype(u_full.dtype), window_strides=(1,),
                                 padding='VALID', dimension_numbers=('NWC', 'WIO', 'NWC'),
                                 feature_group_count=WIDTH_B)
    return y + b_dw


def _layer(x, past_k, past_v, conv_prefix, rel_bias, lam_init, ln1_g, w_in, lq1, lk1, lq2, lk2,
           subln_g, w_dw, b_dw, cln_g, cln_b, w_out, ln2_g, w_up, w_down):
    B, T, _ = x.shape
    h = _rms_norm(x, ln1_g)
    proj = h @ w_in
    q = proj[..., :QK_WIDTH].reshape(B, T, N_HEADS_A, 2 * HEAD_DIM_A)
    k = proj[..., QK_WIDTH:2 * QK_WIDTH].reshape(B, T, N_HEADS_A, 2 * HEAD_DIM_A)
    v = proj[..., 2 * QK_WIDTH:A_COLS].reshape(B, T, N_HEADS_A, V_DIM_A)
    u = proj[..., A_COLS:A_COLS + WIDTH_B] * jax.nn.sigmoid(proj[..., A_COLS + WIDTH_B:])
    if past_k is None:
        past_len, k_all, v_all = 0, k, v
    else:
        past_len = past_k.shape[1]
        k_all = jnp.concatenate([past_k, k], axis=1)
        v_all = jnp.concatenate([past_v, v], axis=1)
    f32 = jnp.float32
    lam = (jnp.exp(jnp.sum(lq1.astype(f32) * lk1.astype(f32)))
           - jnp.exp(jnp.sum(lq2.astype(f32) * lk2.astype(f32))) + lam_init)
    a = _diff_attention(q, k_all, v_all, past_len, rel_bias, lam, subln_g, lam_init)
    u_full = jnp.concatenate([conv_prefix.astype(u.dtype), u], axis=1)
    c = _layer_norm(_causal_depthwise_conv(u_full, w_dw, b_dw), cln_g, cln_b)
    c = c * jax.nn.sigmoid(c)
    x = x + jnp.concatenate([a, c], axis=-1) @ w_out
    h2 = _rms_norm(x, ln2_g)
    x = x + jnp.square(jax.nn.relu(h2 @ w_up)) @ w_down
    return x, k, v, u_full[:, -CONV_PAD:]


def setup_inputs(seed: int = 0) -> dict:
    key = jax.random.key(seed)
    ks = jax.random.split(key, 24)
    n = jax.random.normal
    f = jnp.float32
    H = N_HEADS_A
    return {
        'x_prompt': n(ks[0], (BATCH, SEQ, D_MODEL), f),
        'x_sample': n(ks[1], (DEC_BATCH, DEC_SEQ, D_MODEL), f),
        'cache_k': n(ks[2], (DEPTH, DEC_BATCH, PAST_LEN, H, 2 * HEAD_DIM_A), f),
        'cache_v': n(ks[3], (DEPTH, DEC_BATCH, PAST_LEN, H, V_DIM_A), f),
        'state_conv': 0.5 * n(ks[4], (DEPTH, DEC_BATCH, CONV_PAD, WIDTH_B), f),
        'rel_bias': 0.5 * n(ks[5], (N_BUCKETS, H), f),
        'ln1_g': 1.0 + 0.05 * n(ks[6], (DEPTH, D_MODEL), f),
        'w_in': n(ks[7], (DEPTH, D_MODEL, IN_COLS), f) * D_MODEL ** -0.5,
        'lambda_q1': 0.1 * n(ks[8], (DEPTH, HEAD_DIM_A), f),
        'lambda_k1': 0.1 * n(ks[9], (DEPTH, HEAD_DIM_A), f),
        'lambda_q2': 0.1 * n(ks[10], (DEPTH, HEAD_DIM_A), f),
        'lambda_k2': 0.1 * n(ks[11], (DEPTH, HEAD_DIM_A), f),
        'subln_g': 1.0 + 0.05 * n(ks[12], (DEPTH, V_DIM_A), f),
        'w_dw': n(ks[13], (DEPTH, CONV_WIDTH, WIDTH_B), f) * CONV_WIDTH ** -0.5,
        'b_dw': 0.02 * n(ks[14], (DEPTH, WIDTH_B), f),
        'conv_ln_g': 1.0 + 0.05 * n(ks[15], (DEPTH, WIDTH_B), f),
        'conv_ln_b': 0.02 * n(ks[16], (DEPTH, WIDTH_B), f),
        'w_out': n(ks[17], (DEPTH, WIDTH_A + WIDTH_B, D_MODEL), f) * (WIDTH_A + WIDTH_B) ** -0.5,
        'ln2_g': 1.0 + 0.05 * n(ks[18], (DEPTH, D_MODEL), f),
        'w_up': n(ks[19], (DEPTH, D_MODEL, D_FF), f) * D_MODEL ** -0.5,
        'w_down': n(ks[20], (DEPTH, D_FF, D_MODEL), f) * D_FF ** -0.5,
        'ln_f_g': 1.0 + 0.05 * n(ks[21], (D_MODEL,), f),
    }


def reference(x_prompt, x_sample, cache_k, cache_v, state_conv, rel_bias, ln1_g, w_in,
              lambda_q1, lambda_k1, lambda_q2, lambda_k2, subln_g, w_dw, b_dw, conv_ln_g,
              conv_ln_b, w_out, ln2_g, w_up, w_down, ln_f_g):
    yp, ys = x_prompt, x_sample
    kp, vp, cp, ksm, vsm, csm = [], [], [], [], [], []
    for l in range(DEPTH):
        lam_init = 0.8 - 0.6 * math.exp(-0.3 * l)
        wts = (ln1_g[l], w_in[l], lambda_q1[l], lambda_k1[l], lambda_q2[l], lambda_k2[l],
               subln_g[l], w_dw[l], b_dw[l], conv_ln_g[l], conv_ln_b[l], w_out[l], ln2_g[l],
               w_up[l], w_down[l])
        zero_prefix = jnp.zeros((yp.shape[0], CONV_PAD, WIDTH_B), yp.dtype)
        yp, k1, v1, c1 = _layer(yp, None, None, zero_prefix, rel_bias, lam_init, *wts)
        ys, k2, v2, c2 = _layer(ys, cache_k[l], cache_v[l], state_conv[l], rel_bias, lam_init, *wts)
        kp.append(k1); vp.append(v1); cp.append(c1)
        ksm.append(k2); vsm.append(v2); csm.append(c2)
    yp = _rms_norm(yp, ln_f_g)
    ys = _rms_norm(ys, ln_f_g)
    return (yp, ys, jnp.stack(kp), jnp.stack(vp), jnp.stack(cp), jnp.stack(ksm), jnp.stack(vsm), jnp.stack(csm))
```

```python
import math
from contextlib import ExitStack

import numpy as np
import concourse.bass as bass
import concourse.mybir as mybir
from concourse.bass_utils import run_bass_kernel_spmd

F32 = mybir.dt.float32
BF16 = mybir.dt.bfloat16
ALU = mybir.AluOpType
AF = mybir.ActivationFunctionType
AX = mybir.AxisListType

D = 1024
SEQ = 2048
NSEQ = 2
NSTR = 4
DECT = 32
PAST = 2048
TILE = 512
NCH = 23
EPS = 1e-6
LAM_INIT = 0.8 - 0.6 * math.exp(-0.3 * 0)
NEG = -1e30


class Buf:
    __slots__ = ("name", "w", "r", "dsem", "dcnt")

    def __init__(self, name):
        self.name = name
        self.w = None
        self.r = {}
        self.dsem = None
        self.dcnt = 0


class Eng:
    def __init__(self, name, e, sem, is_pe=False):
        self.name, self.e, self.sem, self.is_pe = name, e, sem, is_pe
        self.cnt = 0
        self.waited = {}


class Tracker:
    def __init__(self, nc, stack):
        self.nc, self.stack = nc, stack
        self.sems, self.engs = {}, {}
        for name, e, is_pe in (("pe", nc.tensor, True), ("act", nc.scalar, False), ("dve", nc.vector, False),
                               ("pool", nc.gpsimd, False), ("sp", nc.sync, False)):
            s = stack.enter_context(nc.semaphore("s_" + name))
            self.sems["E" + name] = s
            self.engs[name] = Eng(name, e, s, is_pe)
        self.ndsem = 0
        self.out_stamps = {}

    def _dsem(self, b, q):
        if b.dsem is None:
            b.dsem = {}
            b.dcnt = {}
        if q not in b.dsem:
            h = self.stack.enter_context(self.nc.semaphore("d%d" % self.ndsem))
            key = "D%d" % self.ndsem
            self.ndsem += 1
            self.sems[key] = h
            b.dsem[q] = key
            b.dcnt[q] = 0
        return b.dsem[q]

    def _wait(self, eng, deps):
        for k, v in deps.items():
            if eng.is_pe and k == "Epe":
                continue
            if eng.waited.get(k, 0) >= v:
                continue
            eng.e.wait_ge(self.sems[k], v)
            eng.waited[k] = v

    @staticmethod
    def _add(d, k, v):
        if d.get(k, 0) < v:
            d[k] = v

    def _deps(self, reads, writes):
        d = {}
        for b in reads:
            if b.w is not None:
                self._add(d, *b.w)
        for b in writes:
            if b.w is not None:
                self._add(d, *b.w)
            for k, v in b.r.items():
                self._add(d, k, v)
        return d

    def _stamp(self, stamp, reads, writes):
        for b in reads:
            self._add(b.r, *stamp)
        for b in writes:
            b.w = stamp
            b.r = {}

    def op(self, en, fn, reads=(), writes=(), inc=True):
        eng = self.engs[en]
        self._wait(eng, self._deps(reads, writes))
        ins = fn()
        if inc:
            eng.cnt += 1
            ins.then_inc(eng.sem, 1)
            stamp = ("E" + en, eng.cnt)
        else:
            stamp = ("E" + en, eng.cnt + 1)
        self._stamp(stamp, reads, writes)
        return ins

    def dma(self, en, out_ap, in_ap, reads=(), writes=(), sem_buf=None, is_output=False, **kw):
        eng = self.engs[en]
        self._wait(eng, self._deps(reads, writes))
        sbuf = sem_buf or (writes[0] if writes else reads[0])
        q = "sw" if en == "pool" else "hw"
        key = self._dsem(sbuf, q)
        ins = eng.e.dma_start(out=out_ap, in_=in_ap, **kw)
        ins.then_inc(self.sems[key], 16)
        sbuf.dcnt[q] += 16
        stamp = (key, sbuf.dcnt[q])
        self._stamp(stamp, reads, writes)
        if is_output:
            self._add(self.out_stamps, *stamp)
        return ins

    def finish(self):
        self._wait(self.engs["sp"], dict(self.out_stamps))


def _bucket_np(rel):
    rel = np.asarray(rel, dtype=np.int64)
    half, max_exact = 16, 8
    n = -rel
    ret = np.where(n < 0, half, 0)
    n = np.abs(n)
    nf = np.maximum(n, 1).astype(np.float32)
    v = np.log(nf / np.float32(max_exact)) / np.float32(math.log(128 / max_exact)) * np.float32(half - max_exact)
    large = np.minimum(max_exact + v.astype(np.int32), half - 1)
    return ret + np.where(n < max_exact, n, large)


def _onehot_const():
    oh = np.zeros((128, 2, 256), np.float32)
    i = np.arange(255)
    for c, cval in enumerate((-128, 0)):
        b = _bucket_np(i - 127 + cval)
        oh[b, c, i] = 1.0
    return oh.reshape(128, 512)


def build_program(debug=False):
    nc = bass.Bass("TRN2", target_bir_lowering=False)
    dt_in = lambda n, s: nc.dram_tensor(n, s, F32, kind="ExternalInput").ap()
    dt_out = lambda n, s: nc.dram_tensor(n, s, F32, kind="ExternalOutput").ap()
    xp = dt_in("xp", [NSEQ * SEQ, D])
    xs = dt_in("xs", [NSTR * DECT, D])
    ck = dt_in("ck", [NSTR, PAST, 512])
    cv = dt_in("cv", [NSTR, PAST, 512])
    scs_in = dt_in("scs", [128, 512])
    w_in = dt_in("w_in", [D, 2560])
    w_out = dt_in("w_out", [D, D])
    w_up = dt_in("w_up", [D, 4096])
    w_down = dt_in("w_down", [4096, D])
    vpa = dt_in("vpa", [128, D])
    vpb = dt_in("vpb", [128, 512])
    vrow = dt_in("vrow", [1, 2048 + 256])
    rbp = dt_in("rbp", [128, 128])
    ident_in = dt_in("ident", [128, 128])
    oh_in = dt_in("oh", [128, 512])
    yp = dt_out("yp", [NSEQ * SEQ, D])
    ys = dt_out("ys", [NSTR * DECT, D])
    kp = dt_out("kp", [NSEQ * SEQ, 512])
    vp = dt_out("vp", [NSEQ * SEQ, 512])
    cpo = dt_out("cpo", [NSEQ, 30, 512])
    kso = dt_out("kso", [NSTR * DECT, 512])
    vso = dt_out("vso", [NSTR * DECT, 512])
    cso = dt_out("cso", [NSTR, 30, 512])
    wscr = nc.dram_tensor("wscr", [NCH, 128, 4096], BF16, kind="Internal").ap()
    gscr = nc.dram_tensor("gscr", [4, 512], F32, kind="Internal").ap()
    dbg = nc.dram_tensor("dbg", [24, 128, D], F32, kind="ExternalOutput").ap() if debug else None

    with ExitStack() as st:
        T = Tracker(nc, st)
        sb = lambda n, s, d: st.enter_context(nc.sbuf_tensor(n, s, d))
        ps = lambda n, s, d: st.enter_context(nc.psum_tensor(n, s, d))

        def act(fn, reads=(), writes=()):
            return T.op("act", fn, reads, writes)

        def dve(fn, reads=(), writes=()):
            return T.op("dve", fn, reads, writes)

        def pool(fn, reads=(), writes=()):
            return T.op("pool", fn, reads, writes)

        def mm(out_ap, outB, pairs, reads, start=True, stop=True):
            n = len(pairs)
            for i, (l, r) in enumerate(pairs):
                T.op("pe", lambda l=l, r=r, i=i: nc.tensor.matmul(out_ap, l, r, start=(start and i == 0),
                                                                  stop=(stop and i == n - 1)),
                     reads=reads, writes=[outB], inc=(i == n - 1))

        def transposes(blocks, reads, outB):
            n = len(blocks)
            for i, (o, a, idn) in enumerate(blocks):
                T.op("pe", lambda o=o, a=a, idn=idn: nc.tensor.transpose(o, a, idn), reads=reads, writes=[outB],
                     inc=(i == n - 1))

        ring = [sb("ring%d" % i, [128, 8, 512], BF16) for i in range(4)]
        ringB = [Buf("ring%d" % i) for i in range(4)]
        scrB = Buf("wscr")
        hff = sb("hff", [128, 32, 512], BF16)
        hffB = [Buf("hff%d" % i) for i in range(8)]
        NX = 5
        X = [sb("x%d" % i, [128, D], F32) for i in range(NX)]
        XB = [Buf("x%d" % i) for i in range(NX)]
        XN = [sb("xn%d" % i, [128, D], BF16) for i in range(2)]
        XNB = [Buf("xn%d" % i) for i in range(2)]
        hT = sb("hT", [128, 8, TILE], BF16); hTB = Buf("hT")
        acT = sb("acT", [128, 8, TILE], BF16); acTB = Buf("acT")
        QTa = sb("QTa", [128, 4, TILE], BF16); QTb = sb("QTb", [128, 4, TILE], BF16); QTB = Buf("QT")
        KTW = PAST + 128
        KT = sb("KT", [128, 4, KTW], BF16); KTB = [Buf("KT%d" % i) for i in range(17)]
        KTS = sb("KTS", [128, 4, 128], BF16); KTSB = Buf("KTS")
        VA = sb("VA", [128, 17, 4, 130], BF16); VAB = [Buf("VA%d" % i) for i in range(17)]
        gBf = sb("gBf", [128, D], F32)
        clng = sb("clng", [128, 512], F32); clnb = sb("clnb", [128, 512], F32)
        constB = Buf("const")
        BiasN = sb("BiasN", [128, 4, 2, 128], F32); BiasB = Buf("BiasN")
        uT = sb("uTb", [128, 4, 30 + TILE], BF16); uTB = Buf("uTb")
        utail = sb("utail", [128, 4, 128], F32); utailB = Buf("utail")
        convo = sb("convo", [128, 4, TILE], F32); convoB = Buf("convo")
        NWK = 6
        WK = [sb("wk%d" % i, [128, 512], F32) for i in range(NWK)]
        WKB = [Buf("wk%d" % i) for i in range(NWK)]
        AO = [sb("ao%d" % i, [128, 512], F32) for i in range(2)]
        AOB = [Buf("ao%d" % i) for i in range(2)]
        PT = [sb("pt%d" % i, [128, 4, 128], BF16) for i in range(4)]
        PTBf = [Buf("pt%d" % i) for i in range(4)]
        AN = [sb("an%d" % i, [128, 512], BF16) for i in range(2)]
        ANB = [Buf("an%d" % i) for i in range(2)]
        CN = [sb("cn%d" % i, [128, 512], BF16) for i in range(2)]
        CNB = [Buf("cn%d" % i) for i in range(2)]
        CTs = [sb("ct%d" % i, [128, 512], F32) for i in range(2)]
        CTBs = [Buf("ct%d" % i) for i in range(2)]
        JKs = [sb("jk%d" % i, [128, 512], F32) for i in range(2)]
        JKBs = [Buf("jk%d" % i) for i in range(2)]
        CT, CTB, JK, JKB = CTs[0], CTBs[0], JKs[0], JKBs[0]
        BGST = [sb("bgst%d" % i, [128, 8], F32) for i in range(6)]
        BGSTB = [Buf("bgst%d" % i) for i in range(6)]
        NST = 12
        STT = [sb("st%d" % i, [128, 8], F32) for i in range(NST)]
        STB = [Buf("st%d" % i) for i in range(NST)]
        ident = sb("identb", [128, 128], BF16); identf = sb("identf", [128, 128], F32)
        gT = sb("gT", [128, 8, 4], F32)
        gac = sb("gac", [128, 8], F32)
        wdwT = sb("wdwT", [128, 4, 32], F32)
        cfar = sb("cfar", [128, 4], F32)
        lamt = sb("lamt", [128, 8], F32)
        epsT = sb("epsT", [128, 1], F32)
        oneT = sb("oneT", [128, 1], F32)
        ohs = WK[3]
        lamb = WK[4]
        rbs = WK[5]
        PB = [ps("pb%d" % i, [128, 512], F32) for i in range(4)]
        PBB = [Buf("pb%d" % i) for i in range(4)]
        PTR = [ps("ptr%d" % i, [128, 8, 128], BF16) for i in range(1)]
        PTRB = [Buf("ptr%d" % i) for i in range(1)]
        BGB = ps("bgb", [128, 512], F32); BGBB = Buf("bgb")
        PO = [ps("po%d" % i, [128, 512], F32) for i in range(2)]
        POB = [Buf("po%d" % i) for i in range(2)]
        gscrB = Buf("gscr")

        rr = {"pb": 0, "wk": 0, "pt": 0, "st": 0, "x": 0, "ptr": 0, "po": 0, "acn": 0}

        def nxt(kind, arr, arrB):
            i = rr[kind] % len(arr)
            rr[kind] += 1
            return arr[i], arrB[i]

        bank = lambda: nxt("pb", PB, PBB)
        wk = lambda: nxt("wk", WK, WKB)
        ptile = lambda: nxt("pt", PT, PTBf)
        stat = lambda: nxt("st", STT, STB)
        xbuf = lambda: nxt("x", X, XB)
        ptr = lambda: nxt("ptr", PTR, PTRB)
        pobank = lambda: nxt("po", PO, POB)
        bgstat = lambda: nxt("acn", BGST, BGSTB)

        def bcast_last(t, pstride, off, mid, n):
            return bass.AP(t, off, [[pstride, 128], [1, mid], [0, n]])

        T.dma("sp", identf[:], ident_in[:], writes=[constB])
        T.dma("sp", ohs[:], oh_in[:], writes=[WKB[3]])
        T.dma("sp", rbs[:, 0:128], rbp[:], writes=[WKB[5]])
        T.dma("sp", WK[0][:, :], vpa[:, 0:512], writes=[WKB[0]])
        T.dma("sp", WK[1][:, :], vpa[:, 512:1024], writes=[WKB[1]])
        T.dma("sp", WK[2][:, :], vpb[:, :], writes=[WKB[2]])
        T.dma("sp", gBf[:], bass.AP(vrow.tensor, 0, [[0, 128], [1, D]]), writes=[constB])
        T.dma("sp", clng[:], bass.AP(vrow.tensor, 1024, [[0, 128], [1, 512]]), writes=[constB])
        T.dma("sp", clnb[:], bass.AP(vrow.tensor, 1536, [[0, 128], [1, 512]]), writes=[constB])
        T.dma("sp", lamb[:, 0:256], bass.AP(vrow.tensor, 2048, [[0, 128], [1, 256]]), writes=[WKB[4]])
        T.dma("sp", cfar[:], bass.AP(rbp.tensor, 15 * 128, [[0, 128], [1, 4]]), writes=[constB])
        pool(lambda: nc.gpsimd.memset(epsT[:], EPS), writes=[constB])
        pool(lambda: nc.gpsimd.memset(oneT[:], 1.0), writes=[constB])
        dve(lambda: nc.vector.tensor_copy(ident[:], identf[:]), reads=[constB], writes=[constB])
        def late_memsets():
            pool(lambda: nc.gpsimd.memset(QTa[:], 0.0), writes=[QTB])
            pool(lambda: nc.gpsimd.memset(QTb[:], 0.0), writes=[QTB])
            pool(lambda: nc.gpsimd.memset(KT[:, :, PAST:KTW], 0.0), writes=[KTB[16]])
            pool(lambda: nc.gpsimd.memset(VA[:, 0:16, :, 128:130], 1.0), writes=VAB[0:16])
            pool(lambda: nc.gpsimd.memset(VA[:, 16, :, :], 0.0), writes=[VAB[16]])
            pool(lambda: nc.gpsimd.memset(VA[0:32, 16, :, 128:130], 1.0), writes=[VAB[16]])
            for i in range(4):
                pool(lambda i=i: nc.gpsimd.memset(PT[i][:], 0.0), writes=[PTBf[i]])

        pool(lambda: nc.gpsimd.memset(gac[:], 1.0), writes=[constB])

        for half in range(2):
            pb, pbB = bank()
            transposes([(pb[:, j * 128:(j + 1) * 128], WK[half][:, j * 128:(j + 1) * 128], identf[:]) for j in range(4)],
                       reads=[WKB[half], constB], outB=pbB)
            act(lambda pb=pb, half=half: nc.scalar.activation(
                out=gT[:, half * 4:(half + 1) * 4, 0:3],
                in_=pb[:, :].rearrange("p (j c) -> p j c", j=4)[:, :, 0:3],
                func=AF.Copy), reads=[pbB], writes=[constB])
        pb, pbB = bank()
        transposes([(pb[:, j * 128:(j + 1) * 128], WK[2][:, j * 128:(j + 1) * 128], identf[:]) for j in range(4)],
                   reads=[WKB[2], constB], outB=pbB)
        act(lambda pb=pb: nc.scalar.activation(out=wdwT[:], in_=pb[:, :].rearrange("p (j c) -> p j c", j=4)[:, :, 0:32],
                                               func=AF.Copy), reads=[pbB], writes=[constB])
        for h in range(4):
            dve(lambda h=h: nc.vector.tensor_scalar(out=gac[:, h:h + 1], in0=gT[:, 0, 2:3], scalar1=1.0 - LAM_INIT,
                                                    scalar2=None, op0=ALU.mult), reads=[constB], writes=[constB])
        dve(lambda: nc.vector.tensor_tensor(out=lamb[:, 0:64], in0=lamb[:, 0:64], in1=lamb[:, 64:128], op=ALU.mult),
            reads=[WKB[4]], writes=[WKB[4]])
        dve(lambda: nc.vector.tensor_tensor(out=lamb[:, 128:192], in0=lamb[:, 128:192], in1=lamb[:, 192:256], op=ALU.mult),
            reads=[WKB[4]], writes=[WKB[4]])
        dve(lambda: nc.vector.reduce_sum(out=lamt[:, 2:3], in_=lamb[:, 0:64], axis=AX.X), reads=[WKB[4]], writes=[constB])
        dve(lambda: nc.vector.reduce_sum(out=lamt[:, 3:4], in_=lamb[:, 128:192], axis=AX.X), reads=[WKB[4]], writes=[constB])
        act(lambda: nc.scalar.activation(out=lamt[:, 4:6], in_=lamt[:, 2:4], func=AF.Exp), reads=[constB], writes=[constB])
        dve(lambda: nc.vector.tensor_tensor(out=lamt[:, 6:7], in0=lamt[:, 4:5], in1=lamt[:, 5:6], op=ALU.subtract),
            reads=[constB], writes=[constB])
        dve(lambda: nc.vector.tensor_scalar(out=lamt[:, 1:2], in0=lamt[:, 6:7], scalar1=LAM_INIT, scalar2=-1.0,
                                            op0=ALU.add, op1=ALU.mult), reads=[constB], writes=[constB])
        bias_late = []

        def build_bias_tables():
            pb, pbB = bank()
            mm(pb[:, :], pbB, [(rbs[:, 0:128], ohs[:, :])], reads=[WKB[5], WKB[3]])
            act(lambda: nc.scalar.activation(out=WK[0][0:4, :], in_=pb[0:4, :], func=AF.Copy), reads=[pbB], writes=[WKB[0]])
            T.dma("sp", gscr[:, :], WK[0][0:4, :], reads=[WKB[0]], writes=[gscrB])
            pool(lambda: nc.gpsimd.memset(BiasN[:], 0.0), writes=[BiasB])
            pool(lambda: nc.gpsimd.memset(BiasN[64:128, :, 1, 0:64], NEG), writes=[BiasB])
            for c in range(2):
                hkc, hkcB = (CT, CTB) if c == 0 else (JK, JKB)
                for h in range(4):
                    T.dma("sp", hkc[:, h * 128:(h + 1) * 128], bass.AP(gscr.tensor, h * 512 + c * 256, [[1, 128], [1, 128]]),
                          reads=[gscrB], writes=[hkcB])
                bias_late.append(lambda c=c, hkc=hkc, hkcB=hkcB: dve(lambda: nc.vector.tensor_tensor(
                    out=BiasN[:, :, c, :], in0=BiasN[:, :, c, :],
                    in1=bass.AP(hkc, 127, [[512, 128], [128, 4], [-1, 128]]), op=ALU.add),
                    reads=[hkcB, BiasB], writes=[BiasB]))

        def chunk_src(cid):
            if cid < 5:
                return w_in.rearrange("(kc p) n -> p kc n", p=128)[:, :, cid * 512:(cid + 1) * 512]
            if cid < 7:
                return w_out.rearrange("(kc p) n -> p kc n", p=128)[:, :, (cid - 5) * 512:(cid - 4) * 512]
            if cid < 15:
                return w_up.rearrange("(kc p) n -> p kc n", p=128)[:, :, (cid - 7) * 512:(cid - 6) * 512]
            fh, g = divmod(cid - 15, 4)
            return w_down.rearrange("(kc p) n -> p kc n", p=128)[:, g * 8:(g + 1) * 8, fh * 512:(fh + 1) * 512]

        wstate = {"issued": 0}
        total_chunks = [0]

        def issue_chunk(gidx):
            tile_i, cid = divmod(gidx, NCH)
            slot = gidx % 4
            if tile_i == 0:
                T.dma("pool", ring[slot][:], chunk_src(cid), writes=[ringB[slot]])
                T.dma("sp", wscr[cid].rearrange("p (k n) -> p k n", k=8), ring[slot][:], reads=[ringB[slot]],
                      writes=[scrB], sem_buf=scrB)
            else:
                T.dma("sp", ring[slot][:], wscr[cid].rearrange("p (k n) -> p k n", k=8), reads=[scrB],
                      writes=[ringB[slot]])

        def chunk(gidx):
            while wstate["issued"] < min(gidx + 3, total_chunks[0]):
                issue_chunk(wstate["issued"])
                wstate["issued"] += 1
            return ring[gidx % 4], ringB[gidx % 4]

        def rstd_from_ss(stt, stB, c0, n, inv_n):
            act(lambda: nc.scalar.activation(out=stt[:, c0 + n:c0 + 2 * n], in_=stt[:, c0:c0 + n], func=AF.Ln,
                                             scale=inv_n, bias=epsT[:, 0:1]), reads=[stB, constB], writes=[stB])
            act(lambda: nc.scalar.activation(out=stt[:, c0 + n:c0 + 2 * n], in_=stt[:, c0 + n:c0 + 2 * n], func=AF.Exp,
                                             scale=-0.5), reads=[stB], writes=[stB])

        def norm_T(xt, NT, gsel, dst, dstB):
            stt, stB = stat()
            pool(lambda: nc.gpsimd.memset(stt[:], 0.0), writes=[stB])
            for t in range(NT):
                x, xB = xt[t]
                xn, xnB = XN[t % 2], XNB[t % 2]
                act(lambda x=x, xn=xn, t=t: nc.scalar.activation(out=xn[:], in_=x[:], func=AF.Square,
                                                                 accum_out=stt[:, t:t + 1]),
                    reads=[xB], writes=[xnB, stB])
                act(lambda t=t: nc.scalar.activation(out=stt[:, 4 + t:5 + t], in_=stt[:, t:t + 1], func=AF.Ln,
                                                     scale=1.0 / D, bias=epsT[:, 0:1]), reads=[stB, constB], writes=[stB])
                act(lambda t=t: nc.scalar.activation(out=stt[:, 4 + t:5 + t], in_=stt[:, 4 + t:5 + t], func=AF.Exp,
                                                     scale=-0.5), reads=[stB], writes=[stB])
                dve(lambda x=x, xn=xn, t=t: nc.vector.tensor_scalar(out=xn[:], in0=x[:], scalar1=stt[:, 4 + t:5 + t],
                                                                    scalar2=None, op0=ALU.mult),
                    reads=[xB, stB], writes=[xnB])
                pt_, ptB_ = ptr()
                transposes([(pt_[:, kc, :], xn[:, kc * 128:(kc + 1) * 128], ident[:]) for kc in range(8)],
                           reads=[xnB, constB], outB=ptB_)
                dve(lambda pt_=pt_, t=t: nc.vector.tensor_tensor(
                    out=dst[:, :, t * 128:(t + 1) * 128], in0=pt_[:, :, :],
                    in1=bass.AP(gT, gsel, [[32, 128], [4, 8], [0, 128]]), op=ALU.mult),
                    reads=[ptB_, constB], writes=[dstB])

        outq = {"n": 0}
        xpre = {}

        def preload_x(gt_, xin_, NT_, nmax=None):
            lst = xpre.setdefault(gt_, [])
            while len(lst) < (NT_ if nmax is None else min(NT_, nmax)):
                t_ = len(lst)
                x, xB = xbuf()
                T.dma("sp", x[:], xin_[t_ * 128:(t_ + 1) * 128, :], writes=[xB])
                lst.append((x, xB))
            return lst
        dbgB = Buf("dbg")

        def dump(idx, ap, n, B):
            if debug:
                T.dma("pool", dbg[idx, :, 0:n], ap, reads=[B], is_output=True)

        def process_tile(gt, kind, xin, yout, kout, vout, ntok, seq_tile, cp_out, next_x=None):
            NT = ntok // 128
            cbase = gt * NCH
            xt = preload_x(gt, xin, NT)
            xpre.pop(gt, None)
            norm_T(xt, NT, 0, hT, hTB)
            def dg_build(cb):
                dve(lambda: nc.vector.tensor_tensor(
                    out=bass.AP(hff, cb * 3968, [[16384, 128], [128, 31], [1, 128]]),
                    in0=bass.AP(ident, 0, [[128, 128], [0, 31], [1, 128]]),
                    in1=bass.AP(wdwT, cb * 32, [[128, 128], [1, 31], [0, 128]]), op=ALU.mult),
                    reads=[constB], writes=hffB)

            for cb_ in range(4):
                dg_build(cb_)
            wq, wqB = chunk(cbase + 0)
            for h in range(4):
                pb, pbB = bank()
                mm(pb[:, :ntok], pbB, [(wq[:, kc, h * 128:(h + 1) * 128], hT[:, kc, :ntok]) for kc in range(8)],
                   reads=[wqB, hTB])
                act(lambda pb=pb, h=h: nc.scalar.activation(out=QTa[0:64, h, :ntok], in_=pb[0:64, :ntok], func=AF.Copy,
                                                            scale=0.125), reads=[pbB], writes=[QTB])
                act(lambda pb=pb, h=h: nc.scalar.activation(out=QTb[64:128, h, :ntok], in_=pb[64:128, :ntok],
                                                            func=AF.Copy, scale=0.125), reads=[pbB], writes=[QTB])
            wkc, wkB_ = chunk(cbase + 1)
            if kind == 'p':
                kt0 = seq_tile * 4
                ktb = KTB[kt0:kt0 + 4]
            for h in range(4):
                pb, pbB = bank()
                mm(pb[:, :ntok], pbB, [(wkc[:, kc, h * 128:(h + 1) * 128], hT[:, kc, :ntok]) for kc in range(8)],
                   reads=[wkB_, hTB])
                if kind == 'p':
                    act(lambda pb=pb, h=h: nc.scalar.activation(out=KT[:, h, kt0 * 128:kt0 * 128 + ntok],
                                                                in_=pb[:, :ntok], func=AF.Copy),
                        reads=[pbB], writes=ktb)
                else:
                    act(lambda pb=pb, h=h: nc.scalar.activation(out=KTS[:, h, :], in_=pb[:, :ntok], func=AF.Copy),
                        reads=[pbB], writes=[KTSB])
            for t in range(NT):
                pb, pbB = bank()
                mm(pb[:, :], pbB, [(hT[:, kc, t * 128:(t + 1) * 128], wkc[:, kc, :]) for kc in range(8)],
                   reads=[wkB_, hTB])
                w, wB = wk()
                act(lambda pb=pb, w=w: nc.scalar.activation(out=w[:], in_=pb[:], func=AF.Copy), reads=[pbB], writes=[wB])
                T.dma("sp", kout[t * 128:(t + 1) * 128, :], w[:], reads=[wB], is_output=True)
            wv, wvB = chunk(cbase + 2)
            vstage = None
            for t in range(NT):
                pb, pbB = bank()
                mm(pb[:, :], pbB, [(hT[:, kc, t * 128:(t + 1) * 128], wv[:, kc, :]) for kc in range(8)],
                   reads=[wvB, hTB])
                w, wB = wk() if kind == 'p' else (AO[1], AOB[1])
                act(lambda pb=pb, w=w: nc.scalar.activation(out=w[:], in_=pb[:], func=AF.Copy), reads=[pbB], writes=[wB])
                T.dma("sp", vout[t * 128:(t + 1) * 128, :], w[:], reads=[wB], is_output=True)
                if kind == 'p':
                    vt = seq_tile * 4 + t
                    pool(lambda w=w, vt=vt: nc.gpsimd.tensor_copy(out=VA[:, vt, :, 0:128],
                                                                  in_=w[:, :].rearrange("p (h d) -> p h d", h=4)),
                         reads=[wB], writes=[VAB[vt]])
                else:
                    vstage = (w, wB)
            wval, wvalB = chunk(cbase + 3)
            wgate, wgateB = chunk(cbase + 4)
            for cb in range(4):
                pa, paB = bank()
                mm(pa[:, :ntok], paB, [(wval[:, kc, cb * 128:(cb + 1) * 128], hT[:, kc, :ntok]) for kc in range(8)],
                   reads=[wvalB, hTB])
                pg, pgB = bank()
                mm(pg[:, :ntok], pgB, [(wgate[:, kc, cb * 128:(cb + 1) * 128], hT[:, kc, :ntok]) for kc in range(8)],
                   reads=[wgateB, hTB])
                sg, sgB = wk()
                act(lambda pg=pg, sg=sg: nc.scalar.activation(out=sg[:, :ntok], in_=pg[:, :ntok], func=AF.Sigmoid),
                    reads=[pgB], writes=[sgB])
                if kind == 's' or cp_out is not None:
                    dve(lambda pa=pa, sg=sg, cb=cb: nc.vector.tensor_tensor(out=utail[:, cb, :], in0=pa[:, ntok - 128:ntok],
                                                                            in1=sg[:, ntok - 128:ntok], op=ALU.mult),
                        reads=[paB, sgB], writes=[utailB])
                if kind == 'p':
                    dve(lambda pa=pa, sg=sg, cb=cb: nc.vector.tensor_tensor(out=uT[:, cb, 30:30 + ntok], in0=pa[:, :ntok],
                                                                            in1=sg[:, :ntok], op=ALU.mult),
                        reads=[paB, sgB], writes=[uTB])
                else:
                    dve(lambda pa=pa, sg=sg, cb=cb: nc.vector.tensor_tensor(
                        out=uT[:, cb, 0:248].rearrange("p (s j) -> p s j", s=4)[:, :, 30:62],
                        in0=pa[:, 0:128].rearrange("p (s j) -> p s j", s=4),
                        in1=sg[:, 0:128].rearrange("p (s j) -> p s j", s=4), op=ALU.mult),
                        reads=[paB, sgB], writes=[uTB])
            if next_x is not None:
                preload_x(gt + 1, next_x[0], next_x[1], nmax=NX - 4)
            def dg(cb, j):
                return bass.AP(hff, cb * 3968 + j * 128, [[16384, 128], [1, 128]])

            def conv_mm(cb, j0, j1):
                if kind == 'p':
                    for j in range(j0, j1):
                        T.op("pe", lambda j=j: nc.tensor.matmul(BGB[:, 0:ntok], dg(cb, j), uT[:, cb, j:j + ntok],
                                                                start=(j == 0), stop=(j == 30)),
                             reads=hffB + [uTB], writes=[BGBB], inc=(j == j1 - 1))
                else:
                    for s_ in range(NSTR):
                        for j in range(j0, j1):
                            T.op("pe", lambda j=j, s_=s_: nc.tensor.matmul(
                                BGB[:, s_ * 32:(s_ + 1) * 32], dg(cb, j), uT[:, cb, s_ * 62 + j:s_ * 62 + j + 32],
                                start=(j == 0), stop=(j == 30)),
                                 reads=hffB + [uTB], writes=[BGBB], inc=(j == j1 - 1 and s_ == NSTR - 1))

            def conv_evac(cb):
                dve(lambda: nc.vector.tensor_scalar(out=convo[:, cb, :ntok], in0=BGB[:, 0:ntok], scalar1=wdwT[:, cb, 31:32],
                                                    scalar2=None, op0=ALU.add), reads=[BGBB, constB], writes=[convoB])

            def conv_state_out():
                if kind == 'p':
                    if cp_out is not None:
                        transposes([(BGB[:, cb * 128:(cb + 1) * 128], utail[:, cb, :], identf[:]) for cb in range(4)],
                                   reads=[utailB, constB], outB=BGBB)
                        act(lambda: nc.scalar.activation(out=CT[:], in_=BGB[:], func=AF.Copy), reads=[BGBB], writes=[CTB])
                        T.dma("sp", cp_out, CT[98:128, :], reads=[CTB], is_output=True)
                    else:
                        pool(lambda: nc.gpsimd.tensor_copy(out=uT[:, :, 0:30], in_=uT[:, :, ntok:ntok + 30]), reads=[uTB],
                             writes=[uTB])
                else:
                    transposes([(BGB[:, cb * 128:(cb + 1) * 128], utail[:, cb, :], identf[:]) for cb in range(4)],
                               reads=[utailB, constB], outB=BGBB)
                    act(lambda: nc.scalar.activation(out=CT[:], in_=BGB[:], func=AF.Copy), reads=[BGBB], writes=[CTB])
                    for s_ in range(NSTR):
                        T.dma("sp", cso[s_], CT[32 * s_ + 2:32 * s_ + 32, :], reads=[CTB], is_output=True)

            P1, P2 = [], []
            if kind == 's':
                pass
            for cb in range(4):
                if kind == 'p':
                    for j0 in range(0, 31, 4):
                        P2.append(lambda cb=cb, j0=j0: conv_mm(cb, j0, min(j0 + 4, 31)))
                else:
                    P2.append(lambda cb=cb: conv_mm(cb, 0, 31))
                P2.append(lambda cb=cb: conv_evac(cb))
            P2.append(conv_state_out)

            def convln_tasks(t):
                cn, cnB = CN[t % 2], CNB[t % 2]
                stt, stB = BGST[t], BGSTB[t]
                CT, CTB, JK, JKB = CTs[t % 2], CTBs[t % 2], JKs[t % 2], JKBs[t % 2]

                def c1a():
                    transposes([(BGB[:, cb * 128:(cb + 1) * 128], convo[:, cb, t * 128:(t + 1) * 128], identf[:])
                                for cb in range(4)], reads=[convoB, constB], outB=BGBB)
                    pool(lambda: nc.gpsimd.memset(stt[:], 0.0), writes=[stB])
                    act(lambda: nc.scalar.activation(out=CT[:], in_=BGB[:], func=AF.Copy, accum_out=stt[:, 0:1]),
                        reads=[BGBB], writes=[CTB, stB])

                def c1b():
                    act(lambda: nc.scalar.activation(out=cn[:], in_=BGB[:], func=AF.Square, accum_out=stt[:, 1:2]),
                        reads=[BGBB], writes=[cnB, stB])

                def c2():
                    dve(lambda: nc.vector.tensor_scalar(out=stt[:, 2:3], in0=stt[:, 0:1], scalar1=1.0 / 512, scalar2=None,
                                                        op0=ALU.mult), reads=[stB], writes=[stB])
                    dve(lambda: nc.vector.tensor_tensor(out=stt[:, 3:4], in0=stt[:, 2:3], in1=stt[:, 2:3], op=ALU.mult),
                        reads=[stB], writes=[stB])
                    dve(lambda: nc.vector.scalar_tensor_tensor(out=stt[:, 4:5], in0=stt[:, 1:2], scalar=1.0 / 512,
                                                               in1=stt[:, 3:4], op0=ALU.mult, op1=ALU.subtract),
                        reads=[stB], writes=[stB])

                def c3():
                    rstd_from_ss(stt, stB, 4, 1, 1.0)

                def c4a():
                    dve(lambda: nc.vector.scalar_tensor_tensor(out=stt[:, 6:7], in0=stt[:, 2:3], scalar=-1.0, in1=stt[:, 5:6],
                                                               op0=ALU.mult, op1=ALU.mult), reads=[stB], writes=[stB])
                    dve(lambda: nc.vector.tensor_scalar(out=CT[:], in0=CT[:], scalar1=stt[:, 5:6], scalar2=stt[:, 6:7],
                                                        op0=ALU.mult, op1=ALU.add), reads=[CTB, stB], writes=[CTB])

                def c5a():
                    dve(lambda: nc.vector.tensor_tensor(out=CT[:], in0=CT[:], in1=clng[:], op=ALU.mult),
                        reads=[CTB, constB], writes=[CTB])

                def c5b():
                    dve(lambda: nc.vector.tensor_tensor(out=CT[:], in0=CT[:], in1=clnb[:], op=ALU.add),
                        reads=[CTB, constB], writes=[CTB])

                def c6a():
                    act(lambda: nc.scalar.activation(out=JK[:], in_=CT[:], func=AF.Exp, scale=-1.0), reads=[CTB], writes=[JKB])

                def c6b():
                    act(lambda: nc.scalar.activation(out=JK[:], in_=JK[:], func=AF.Ln, bias=oneT[:, 0:1]), reads=[JKB, constB],
                        writes=[JKB])

                def c6c():
                    act(lambda: nc.scalar.activation(out=JK[:], in_=JK[:], func=AF.Exp, scale=-1.0), reads=[JKB], writes=[JKB])

                def c7():
                    dve(lambda: nc.vector.tensor_tensor(out=cn[:], in0=CT[:], in1=JK[:], op=ALU.mult),
                        reads=[CTB, JKB], writes=[cnB])

                def c8():
                    pt_, ptB_ = ptr()
                    transposes([(pt_[:, kc, :], cn[:, kc * 128:(kc + 1) * 128], ident[:]) for kc in range(4)],
                               reads=[cnB, constB], outB=ptB_)
                    dve(lambda: nc.vector.tensor_copy(out=acT[:, 4:8, t * 128:(t + 1) * 128], in_=pt_[:, 0:4, :]),
                        reads=[ptB_], writes=[acTB])
                return [c1a, c1b, c2, c3, c4a, c5a, c5b, c6a, c6b, c6c, c7], c8

            def subln_tasks(t, ao, aoB):
                an, anB = AN[t % 2], ANB[t % 2]
                stt, stB = BGST[4 + t % 2], BGSTB[4 + t % 2]

                def s1(h):
                    if h == 0:
                        pool(lambda: nc.gpsimd.memset(stt[:], 0.0), writes=[stB])
                    act(lambda h=h: nc.scalar.activation(out=an[:, h * 128:(h + 1) * 128],
                                                         in_=ao[:, h * 128:(h + 1) * 128], func=AF.Square,
                                                         accum_out=stt[:, h:h + 1]), reads=[aoB], writes=[anB, stB])

                def s2():
                    rstd_from_ss(stt, stB, 0, 4, 1.0 / 128)

                def s3():
                    dve(lambda: nc.vector.tensor_tensor(out=an[:, :].rearrange("p (h d) -> p h d", h=4),
                                                        in0=ao[:, :].rearrange("p (h d) -> p h d", h=4),
                                                        in1=bass.AP(stt, 4, [[8, 128], [1, 4], [0, 128]]), op=ALU.mult),
                        reads=[aoB, stB], writes=[anB])

                def s4():
                    pt_, ptB_ = ptr()
                    transposes([(pt_[:, kc, :], an[:, kc * 128:(kc + 1) * 128], ident[:]) for kc in range(4)],
                               reads=[anB, constB], outB=ptB_)
                    dve(lambda: nc.vector.tensor_tensor(out=acT[:, 0:4, t * 128:(t + 1) * 128], in0=pt_[:, 0:4, :],
                                                        in1=bass.AP(gac, 0, [[8, 128], [1, 4], [0, 128]]), op=ALU.mult),
                        reads=[ptB_, constB], writes=[acTB])
                return [lambda: s1(0), lambda: s1(1), lambda: s1(2), lambda: s1(3), s2, s3, s4]

            def run_tasks(q, k):
                for _ in range(min(k, len(q))):
                    q.pop(0)()

            while bias_late:
                bias_late.pop(0)()

            def attn_items(gb):
                items = []
                for h in range(4):
                    for m in range(2):
                        far = list(range(0, max(gb - 1, 0)))
                        near = ([gb - 1] if gb >= 1 else []) + [gb]
                        groups = [('near', near)] + [('far', far[i:i + 4]) for i in range(0, len(far), 4)]
                        for gi, (kd, tl) in enumerate(groups):
                            items.append(dict(h=h, m=m, kind=kd, tiles=tl, first=(gi == 0), last=(gi == len(groups) - 1)))
                return items

            def attn_multi(blocks, weave=None):
                items = []
                for bi, blk in enumerate(blocks):
                    its = attn_items(blk['gb'])
                    for k_, it in enumerate(its):
                        it['blk'] = bi
                        it['left'] = len(its) - k_
                        it['first_in_blk'] = (k_ == 0)
                    items.extend(its)
                n = len(items)
                LAG = 3
                state = {}
                deferred = []

                def emit_qk(it):
                    blk = blocks[it['blk']]
                    qc0, nq, qoff = blk['qc0'], blk['nq'], blk['qoff']
                    h, m = it['h'], it['m']
                    S, SB = bank()
                    it['S'], it['SB'] = S, SB
                    QTm = QTa if m == 0 else QTb
                    nt_ = len(it['tiles'])
                    for idx, a in enumerate(it['tiles']):
                        T.op("pe", lambda idx=idx, a=a: nc.tensor.matmul(
                            S[:, idx * 128 + qoff: idx * 128 + qoff + nq], KT[:, h, a * 128:(a + 1) * 128],
                            QTm[:, h, qc0:qc0 + nq], start=True, stop=True),
                             reads=[KTB[a], QTB], writes=[SB], inc=(idx == nt_ - 1))

                def emit_mid(it):
                    blk = blocks[it['blk']]
                    nq, qoff = blk['nq'], blk['qoff']
                    h, m = it['h'], it['m']
                    S, SB = it['S'], it['SB']
                    nt_ = len(it['tiles'])
                    P, PBf = ptile()
                    it['P'], it['PBf'] = P, PBf
                    Sv = S[:, 0:nt_ * 128].rearrange("p (g q) -> p g q", g=nt_)[:, :, qoff:qoff + nq]
                    if it['kind'] == 'far':
                        act(lambda: nc.scalar.activation(out=P[:, 0:nt_, qoff:qoff + nq], in_=Sv, func=AF.Exp,
                                                         bias=cfar[:, h:h + 1]), reads=[SB, constB], writes=[PBf])
                    else:
                        sn, snB, snv = it['sn']
                        act(lambda: nc.scalar.activation(out=P[:, 0:nt_, qoff:qoff + nq], in_=snv, func=AF.Exp),
                            reads=[snB], writes=[PBf])

                def emit_add(it):
                    blk = blocks[it['blk']]
                    nq, qoff = blk['nq'], blk['qoff']
                    h = it['h']
                    S, SB = it['S'], it['SB']
                    nt_ = len(it['tiles'])
                    Sv = S[:, 0:nt_ * 128].rearrange("p (g q) -> p g q", g=nt_)[:, :, qoff:qoff + nq]
                    sn, snB = wk()
                    snv = sn[:, 0:nt_ * 128].rearrange("p (g q) -> p g q", g=nt_)[:, :, 0:nq]
                    c0 = 2 - nt_
                    dve(lambda: nc.vector.tensor_tensor(out=snv, in0=Sv, in1=BiasN[:, h, c0:2, 0:nq], op=ALU.add),
                        reads=[SB, BiasB], writes=[snB])
                    it['sn'] = (sn, snB, snv)

                def emit_pv(it):
                    blk = blocks[it['blk']]
                    nq, qoff, ao, aoB = blk['nq'], blk['qoff'], blk['ao'], blk['aoB']
                    h, m = it['h'], it['m']
                    nt_ = len(it['tiles'])
                    P, PBf = it['P'], it['PBf']
                    if it['first'] and m == 0:
                        state['po'] = pobank()
                    O, OB = state['po']
                    for idx, a in enumerate(it['tiles']):
                        T.op("pe", lambda idx=idx, a=a: nc.tensor.matmul(
                            O[:, m * 256:m * 256 + 129], P[:, idx, :], VA[:, a, h, 0:129],
                            start=(it['first'] and idx == 0), stop=(it['last'] and idx == nt_ - 1)),
                             reads=[PBf, VAB[a]], writes=[OB], inc=(idx == nt_ - 1))
                    if it['last'] and m == 1:
                        r0, r1 = qoff, qoff + nq

                        def fin(O=O, OB=OB, h=h, blk=blk):
                            stt, stB = stat()
                            dve(lambda: nc.vector.reciprocal(out=stt[r0:r1, 0:2],
                                                             in_=bass.AP(O, r0 * 512 + 128, [[512, r1 - r0], [256, 2]])),
                                reads=[OB], writes=[stB])
                            t2, t2B = wk()
                            dve(lambda: nc.vector.tensor_scalar(out=t2[r0:r1, 0:128], in0=O[r0:r1, 256:384],
                                                                scalar1=stt[r0:r1, 1:2], scalar2=lamt[r0:r1, 1:2],
                                                                op0=ALU.mult, op1=ALU.mult),
                                reads=[OB, stB, constB], writes=[t2B])
                            dve(lambda: nc.vector.scalar_tensor_tensor(out=ao[r0:r1, h * 128:(h + 1) * 128],
                                                                       in0=O[r0:r1, 0:128], scalar=stt[r0:r1, 0:1],
                                                                       in1=t2[r0:r1, 0:128], op0=ALU.mult, op1=ALU.add),
                                reads=[OB, stB, t2B], writes=[aoB])
                            if h == 3 and blk.get('on_done') is not None:
                                blk['on_done']()
                        deferred.append([2, fin])

                def tick():
                    for d in list(deferred):
                        d[0] -= 1
                        if d[0] <= 0:
                            deferred.remove(d)
                            d[1]()

                for i in range(n + LAG):
                    if i < n:
                        emit_qk(items[i])
                    if 0 <= i - 1 < n and items[i - 1]['kind'] == 'near':
                        emit_add(items[i - 1])
                    j = i - LAG
                    if j >= 0:
                        it = items[j]
                        if it['first_in_blk'] and it['blk'] >= 2:
                            while P1 and P1[0][0] <= it['blk'] - 2:
                                P1.pop(0)[1]()
                        emit_mid(it)
                        if j >= 1:
                            tick()
                            emit_pv(items[j - 1])
                        if weave is not None:
                            weave(it['left'], n - j)
                tick()
                emit_pv(items[n - 1])
                for d in list(deferred):
                    d[1]()

            if kind == 'p':
                for t0_ in range(0, NT, 2):
                    ch0, c80 = convln_tasks(t0_)
                    ch1, c81 = convln_tasks(t0_ + 1)
                    P2.extend(ch0[0:2])
                    P2.extend(ch1[0:2])
                    for f0, f1 in zip(ch0[2:], ch1[2:]):
                        P2.append(f0)
                        P2.append(f1)
                    P2.append(c80)
                    P2.append(c81)
                blocks = []
                for t in range(NT):
                    ao, aoB = AO[t % 2], AOB[t % 2]

                    def on_done(t=t, ao=ao, aoB=aoB):
                        if gt == 0: dump(t, ao[:], 512, aoB)
                        P1.extend([(t, f) for f in subln_tasks(t, ao, aoB)])
                    blocks.append(dict(qc0=t * 128, nq=128, qoff=0, gb=seq_tile * 4 + t, ao=ao, aoB=aoB, on_done=on_done))

                def weave(left_in_block, left_total):
                    for _ in range(min(len(P1), -(-len(P1) // max(left_in_block, 1)))):
                        P1.pop(0)[1]()
                    run_tasks(P2, -(-len(P2) // max(left_total, 1)))

                attn_multi(blocks, weave)
                run_tasks(P2, len(P2))
                while P1:
                    P1.pop(0)[1]()
            else:
                run_tasks(P2, len(P2))
                ao, aoB = AO[0], AOB[0]
                vw, vwB = vstage
                def load_kc(s_):
                    hb = 16 * (s_ % 2)
                    T.dma("pool", hff[:, hb:hb + 16, :], ck[s_].rearrange("(t p) n -> p t n", p=128),
                          writes=hffB[hb // 4:hb // 4 + 4])

                load_kc(0)
                for s in range(NSTR):
                    hb = 16 * (s % 2)
                    if s + 1 < NSTR:
                        load_kc(s + 1)
                    for h in range(4):
                        T.dma("pool", VA[:, 0:16, h, 0:128],
                              cv[s][:, h * 128:(h + 1) * 128].rearrange("(t p) d -> p t d", p=128), writes=VAB[0:16])
                    for a2 in range(8):
                        pt_, ptB_ = ptr()
                        blocks = []
                        for h in range(4):
                            for aa in range(2):
                                blocks.append((pt_[:, h * 2 + aa, :], hff[:, hb + a2 * 2 + aa, h * 128:(h + 1) * 128], ident[:]))
                        transposes(blocks, reads=[hffB[(hb + a2 * 2) // 4], constB], outB=ptB_)
                        act(lambda pt_=pt_, a2=a2: nc.scalar.activation(
                            out=KT[:, :, a2 * 256:(a2 + 1) * 256],
                            in_=pt_[:, :, :].rearrange("p (h a) q -> p h (a q)", h=4), func=AF.Copy),
                            reads=[ptB_], writes=KTB[a2 * 2:a2 * 2 + 2])
                    pool(lambda s=s: nc.gpsimd.tensor_copy(out=KT[:, :, PAST:PAST + 32], in_=KTS[:, :, 32 * s:32 * s + 32]),
                         reads=[KTSB], writes=[KTB[16]])
                    T.dma("pool", VA[0:32, 16, :, 0:128], vw[32 * s:32 * s + 32, :].rearrange("p (h d) -> p h d", h=4),
                          reads=[vwB], writes=[VAB[16]])
                    attn_multi([dict(qc0=32 * s, nq=32, qoff=32 * s, gb=16, ao=ao, aoB=aoB, on_done=None)])
                chain, c8_ = convln_tasks(0)
                for f in chain + [c8_] + subln_tasks(0, ao, aoB):
                    f()

            for fh in range(2):
                wo, woB = chunk(cbase + 5 + fh)
                for t in range(NT):
                    pb, pbB = bank()
                    mm(pb[:, :], pbB, [(acT[:, kc, t * 128:(t + 1) * 128], wo[:, kc, :]) for kc in range(8)],
                       reads=[woB, acTB])
                    x, xB = xt[t]
                    dve(lambda pb=pb, x=x, fh=fh: nc.vector.tensor_tensor(out=x[:, fh * 512:(fh + 1) * 512], in0=pb[:],
                                                                          in1=x[:, fh * 512:(fh + 1) * 512], op=ALU.add),
                        reads=[pbB, xB], writes=[xB])
            if gt == 0:
                for t in range(NT): dump(8 + t, xt[t][0][:], 1024, xt[t][1])
            norm_T(xt, NT, 1, hT, hTB)
            for c in range(8):
                wu, wuB = chunk(cbase + 7 + c)
                for j in range(4):
                    pb, pbB = bank()
                    mm(pb[:, :ntok], pbB, [(wu[:, kc, j * 128:(j + 1) * 128], hT[:, kc, :ntok]) for kc in range(8)],
                       reads=[wuB, hTB])
                    r, rB = wk()
                    act(lambda pb=pb, r=r: nc.scalar.activation(out=r[:, :ntok], in_=pb[:, :ntok], func=AF.Relu),
                        reads=[pbB], writes=[rB])
                    act(lambda r=r, c=c, j=j: nc.scalar.activation(out=hff[:, c * 4 + j, :ntok], in_=r[:, :ntok], func=AF.Square),
                        reads=[rB], writes=[hffB[c]])
            stt, stB = stat()
            pool(lambda: nc.gpsimd.memset(stt[:], 0.0), writes=[stB])
            for fh in range(2):
                accs = [bank() for _ in range(NT)]
                for g in range(4):
                    wd, wdB = chunk(cbase + 15 + fh * 4 + g)
                    for t in range(NT):
                        acc, accB = accs[t]
                        mm(acc[:, :], accB, [(hff[:, g * 8 + kc, t * 128:(t + 1) * 128], wd[:, kc, :]) for kc in range(8)],
                           reads=[wdB, hffB[2 * g], hffB[2 * g + 1]], start=(g == 0), stop=(g == 3))
                for t in range(NT):
                    acc, accB = accs[t]
                    x, xB = xt[t]
                    dve(lambda acc=acc, x=x, fh=fh: nc.vector.tensor_tensor(out=x[:, fh * 512:(fh + 1) * 512], in0=acc[:],
                                                                            in1=x[:, fh * 512:(fh + 1) * 512], op=ALU.add),
                        reads=[accB, xB], writes=[xB])
            if gt == 0:
                for t in range(NT): dump(12 + t, xt[t][0][:], 1024, xt[t][1])
            for t in range(NT):
                x, xB = xt[t]
                xn, xnB = XN[t % 2], XNB[t % 2]
                act(lambda x=x, xn=xn, t=t: nc.scalar.activation(out=xn[:], in_=x[:], func=AF.Square,
                                                                 accum_out=stt[:, t:t + 1]), reads=[xB], writes=[xnB, stB])
            rstd_from_ss(stt, stB, 0, 4, 1.0 / D)
            for t in range(NT):
                x, xB = xt[t]
                dve(lambda x=x, t=t: nc.vector.scalar_tensor_tensor(out=x[:], in0=x[:], scalar=stt[:, 4 + t:5 + t], in1=gBf[:],
                                                                    op0=ALU.mult, op1=ALU.mult),
                    reads=[xB, stB, constB], writes=[xB])
                T.dma("sp", yout[t * 128:(t + 1) * 128, :], x[:], reads=[xB], is_output=True)

        n_ptiles = NSEQ * (SEQ // TILE)
        total_chunks[0] = (n_ptiles + 1) * NCH
        gt = 0
        preload_x(0, xp[0:TILE, :], TILE // 128)
        chunk(0)
        late_memsets()
        build_bias_tables()
        for sq in range(NSEQ):
            pool(lambda: nc.gpsimd.memset(uT[:, :, 0:30], 0.0), writes=[uTB])
            for ti in range(SEQ // TILE):
                r0 = sq * SEQ + ti * TILE
                last = (ti == SEQ // TILE - 1)
                if gt + 1 < n_ptiles:
                    nx = (xp[r0 + TILE:r0 + 2 * TILE, :], TILE // 128)
                else:
                    nx = (xs, 1)
                process_tile(gt, 'p', xp[r0:r0 + TILE, :], yp[r0:r0 + TILE, :], kp[r0:r0 + TILE, :], vp[r0:r0 + TILE, :],
                             TILE, ti, cpo[sq] if last else None, next_x=nx)
                gt += 1
        w, wB = wk()
        T.dma("sp", w[:], scs_in[:], writes=[wB])
        pb, pbB = bank()
        transposes([(pb[:, cb * 128:(cb + 1) * 128], w[:, cb * 128:(cb + 1) * 128], identf[:]) for cb in range(4)],
                   reads=[wB, constB], outB=pbB)
        for cb in range(4):
            act(lambda cb=cb: nc.scalar.activation(
                out=uT[:, cb, 0:248].rearrange("p (s j) -> p s j", s=4)[:, :, 0:30],
                in_=pb[:, cb * 128:(cb + 1) * 128].rearrange("p (s j) -> p s j", s=4)[:, :, 0:30], func=AF.Copy),
                reads=[pbB], writes=[uTB])
        process_tile(gt, 's', xs, ys, kso, vso, 128, None, None)
        T.finish()
    return nc


_PROG = {}


def kernel(x_prompt, x_sample, cache_k, cache_v, state_conv, rel_bias, ln1_g, w_in, lambda_q1, lambda_k1,
           lambda_q2, lambda_k2, subln_g, w_dw, b_dw, conv_ln_g, conv_ln_b, w_out, ln2_g, w_up, w_down, ln_f_g):
    f = lambda a: np.ascontiguousarray(np.asarray(a, dtype=np.float32))
    n = 8
    x_prompt, x_sample, cache_k, cache_v, state_conv = map(f, (x_prompt, x_sample, cache_k, cache_v, state_conv))
    vpa = np.zeros((128, D), np.float32)
    vpa[0] = f(ln1_g)[0]; vpa[1] = f(ln2_g)[0]; vpa[2, 0:128] = f(subln_g)[0]
    vpb = np.zeros((128, 512), np.float32)
    vpb[0:31] = f(w_dw)[0]; vpb[31] = f(b_dw)[0]
    vrow = np.concatenate([f(ln_f_g).reshape(-1), f(conv_ln_g)[0], f(conv_ln_b)[0], f(lambda_q1)[0], f(lambda_k1)[0],
                           f(lambda_q2)[0], f(lambda_k2)[0]]).reshape(1, -1).astype(np.float32)
    rbp = np.zeros((128, 128), np.float32)
    rbp[0:32, 0:4] = f(rel_bias)
    ident = np.eye(128, dtype=np.float32)
    oh = _onehot_const()
    shared = dict(w_in=f(w_in)[0], w_out=f(w_out)[0], w_up=f(w_up)[0], w_down=f(w_down)[0], vpa=vpa, vpb=vpb, vrow=vrow,
                  rbp=rbp, ident=ident, oh=oh)
    in_maps = []
    for c in range(n):
        scs = np.zeros((NSTR, 32, 512), np.float32)
        scs[:, 0:30] = state_conv[0, NSTR * c:NSTR * (c + 1)]
        m = dict(shared)
        m.update(xp=x_prompt[NSEQ * c:NSEQ * (c + 1)].reshape(NSEQ * SEQ, D),
                 xs=x_sample[NSTR * c:NSTR * (c + 1)].reshape(NSTR * DECT, D),
                 ck=cache_k[0, NSTR * c:NSTR * (c + 1)].reshape(NSTR, PAST, 512),
                 cv=cache_v[0, NSTR * c:NSTR * (c + 1)].reshape(NSTR, PAST, 512),
                 scs=scs.reshape(128, 512))
        in_maps.append(m)
    if "nc" not in _PROG:
        _PROG["nc"] = build_program()
    res = run_bass_kernel_spmd(_PROG["nc"], in_maps, core_ids=list(range(n)))
    R = res.results
    cat = lambda k: np.concatenate([np.asarray(r[k], dtype=np.float32) for r in R], axis=0)
    y_prompt = cat("yp").reshape(16, SEQ, D)
    y_sample = cat("ys").reshape(32, DECT, D)
    k_prompt = cat("kp").reshape(1, 16, SEQ, 4, 128)
    v_prompt = cat("vp").reshape(1, 16, SEQ, 4, 128)
    conv_prompt = cat("cpo").reshape(1, 16, 30, 512)
    k_sample = cat("kso").reshape(1, 32, DECT, 4, 128)
    v_sample = cat("vso").reshape(1, 32, DECT, 4, 128)
    conv_sample = cat("cso").reshape(1, 32, 30, 512)
    return (y_prompt, y_sample, k_prompt, v_prompt, conv_prompt, k_sample, v_sample, conv_sample)
```

```python
import math
from contextlib import ExitStack

import numpy as np
import concourse.bass as bass
import concourse.mybir as mybir
from concourse.bass_utils import run_bass_kernel_spmd

F32 = mybir.dt.float32
BF16 = mybir.dt.bfloat16
ALU = mybir.AluOpType
AF = mybir.ActivationFunctionType
AX = mybir.AxisListType

D = 1024
SEQ = 2048
NSEQ = 2
NSTR = 4
DECT = 32
PAST = 2048
TILE = 512
NCH = 23
EPS = 1e-6
LAM_INIT = 0.8 - 0.6 * math.exp(-0.3 * 0)
NEG = -1e30


class Buf:
    __slots__ = ("name", "w", "r", "dsem", "dcnt")

    def __init__(self, name):
        self.name = name
        self.w = None
        self.r = {}
        self.dsem = None
        self.dcnt = 0


class Eng:
    def __init__(self, name, e, sem, is_pe=False):
        self.name, self.e, self.sem, self.is_pe = name, e, sem, is_pe
        self.cnt = 0
        self.waited = {}


class Tracker:
    def __init__(self, nc, stack):
        self.nc, self.stack = nc, stack
        self.sems, self.engs = {}, {}
        for name, e, is_pe in (("pe", nc.tensor, True), ("act", nc.scalar, False), ("dve", nc.vector, False),
                               ("pool", nc.gpsimd, False), ("sp", nc.sync, False)):
            s = stack.enter_context(nc.semaphore("s_" + name))
            self.sems["E" + name] = s
            self.engs[name] = Eng(name, e, s, is_pe)
        self.ndsem = 0
        self.out_stamps = {}

    def _dsem(self, b, q):
        if b.dsem is None:
            b.dsem = {}
            b.dcnt = {}
        if q not in b.dsem:
            h = self.stack.enter_context(self.nc.semaphore("d%d" % self.ndsem))
            key = "D%d" % self.ndsem
            self.ndsem += 1
            self.sems[key] = h
            b.dsem[q] = key
            b.dcnt[q] = 0
        return b.dsem[q]

    def _wait(self, eng, deps):
        for k, v in deps.items():
            if eng.is_pe and k == "Epe":
                continue
            if eng.waited.get(k, 0) >= v:
                continue
            eng.e.wait_ge(self.sems[k], v)
            eng.waited[k] = v

    @staticmethod
    def _add(d, k, v):
        if d.get(k, 0) < v:
            d[k] = v

    def _deps(self, reads, writes):
        d = {}
        for b in reads:
            if b.w is not None:
                self._add(d, *b.w)
        for b in writes:
            if b.w is not None:
                self._add(d, *b.w)
            for k, v in b.r.items():
                self._add(d, k, v)
        return d

    def _stamp(self, stamp, reads, writes):
        for b in reads:
            self._add(b.r, *stamp)
        for b in writes:
            b.w = stamp
            b.r = {}

    def op(self, en, fn, reads=(), writes=(), inc=True):
        eng = self.engs[en]
        self._wait(eng, self._deps(reads, writes))
        ins = fn()
        if inc:
            eng.cnt += 1
            ins.then_inc(eng.sem, 1)
            stamp = ("E" + en, eng.cnt)
        else:
            stamp = ("E" + en, eng.cnt + 1)
        self._stamp(stamp, reads, writes)
        return ins

    def dma(self, en, out_ap, in_ap, reads=(), writes=(), sem_buf=None, is_output=False, **kw):
        eng = self.engs[en]
        self._wait(eng, self._deps(reads, writes))
        sbuf = sem_buf or (writes[0] if writes else reads[0])
        q = "sw" if en == "pool" else "hw"
        key = self._dsem(sbuf, q)
        ins = eng.e.dma_start(out=out_ap, in_=in_ap, **kw)
        ins.then_inc(self.sems[key], 16)
        sbuf.dcnt[q] += 16
        stamp = (key, sbuf.dcnt[q])
        self._stamp(stamp, reads, writes)
        if is_output:
            self._add(self.out_stamps, *stamp)
        return ins

    def finish(self):
        self._wait(self.engs["sp"], dict(self.out_stamps))


def _bucket_np(rel):
    rel = np.asarray(rel, dtype=np.int64)
    half, max_exact = 16, 8
    n = -rel
    ret = np.where(n < 0, half, 0)
    n = np.abs(n)
    nf = np.maximum(n, 1).astype(np.float32)
    v = np.log(nf / np.float32(max_exact)) / np.float32(math.log(128 / max_exact)) * np.float32(half - max_exact)
    large = np.minimum(max_exact + v.astype(np.int32), half - 1)
    return ret + np.where(n < max_exact, n, large)


def _onehot_const():
    oh = np.zeros((128, 2, 256), np.float32)
    i = np.arange(255)
    for c, cval in enumerate((-128, 0)):
        b = _bucket_np(i - 127 + cval)
        oh[b, c, i] = 1.0
    return oh.reshape(128, 512)


def build_program(debug=False):
    nc = bass.Bass("TRN2", target_bir_lowering=False)
    dt_in = lambda n, s: nc.dram_tensor(n, s, F32, kind="ExternalInput").ap()
    dt_out = lambda n, s: nc.dram_tensor(n, s, F32, kind="ExternalOutput").ap()
    xp = dt_in("xp", [NSEQ * SEQ, D])
    xs = dt_in("xs", [NSTR * DECT, D])
    ck = dt_in("ck", [NSTR, PAST, 512])
    cv = dt_in("cv", [NSTR, PAST, 512])
    scs_in = dt_in("scs", [128, 512])
    w_in = dt_in("w_in", [D, 2560])
    w_out = dt_in("w_out", [D, D])
    w_up = dt_in("w_up", [D, 4096])
    w_down = dt_in("w_down", [4096, D])
    vpa = dt_in("vpa", [128, D])
    vpb = dt_in("vpb", [128, 512])
    vrow = dt_in("vrow", [1, 2048 + 256])
    rbp = dt_in("rbp", [128, 128])
    ident_in = dt_in("ident", [128, 128])
    oh_in = dt_in("oh", [128, 512])
    yp = dt_out("yp", [NSEQ * SEQ, D])
    ys = dt_out("ys", [NSTR * DECT, D])
    kp = dt_out("kp", [NSEQ * SEQ, 512])
    vp = dt_out("vp", [NSEQ * SEQ, 512])
    cpo = dt_out("cpo", [NSEQ, 30, 512])
    kso = dt_out("kso", [NSTR * DECT, 512])
    vso = dt_out("vso", [NSTR * DECT, 512])
    cso = dt_out("cso", [NSTR, 30, 512])
    wscr = nc.dram_tensor("wscr", [NCH, 128, 4096], BF16, kind="Internal").ap()
    gscr = nc.dram_tensor("gscr", [4, 512], F32, kind="Internal").ap()
    dbg = nc.dram_tensor("dbg", [24, 128, D], F32, kind="ExternalOutput").ap() if debug else None

    with ExitStack() as st:
        T = Tracker(nc, st)
        sb = lambda n, s, d: st.enter_context(nc.sbuf_tensor(n, s, d))
        ps = lambda n, s, d: st.enter_context(nc.psum_tensor(n, s, d))

        def act(fn, reads=(), writes=()):
            return T.op("act", fn, reads, writes)

        def dve(fn, reads=(), writes=()):
            return T.op("dve", fn, reads, writes)

        def pool(fn, reads=(), writes=()):
            return T.op("pool", fn, reads, writes)

        def mm(out_ap, outB, pairs, reads, start=True, stop=True):
            n = len(pairs)
            for i, (l, r) in enumerate(pairs):
                T.op("pe", lambda l=l, r=r, i=i: nc.tensor.matmul(out_ap, l, r, start=(start and i == 0),
                                                                  stop=(stop and i == n - 1)),
                     reads=reads, writes=[outB], inc=(i == n - 1))

        def transposes(blocks, reads, outB):
            n = len(blocks)
            for i, (o, a, idn) in enumerate(blocks):
                T.op("pe", lambda o=o, a=a, idn=idn: nc.tensor.transpose(o, a, idn), reads=reads, writes=[outB],
                     inc=(i == n - 1))

        ring = [sb("ring%d" % i, [128, 8, 512], BF16) for i in range(4)]
        ringB = [Buf("ring%d" % i) for i in range(4)]
        scrB = Buf("wscr")
        hff = sb("hff", [128, 32, 512], BF16)
        hffB = [Buf("hff%d" % i) for i in range(8)]
        NX = 5
        X = [sb("x%d" % i, [128, D], F32) for i in range(NX)]
        XB = [Buf("x%d" % i) for i in range(NX)]
        XN = [sb("xn%d" % i, [128, D], BF16) for i in range(2)]
        XNB = [Buf("xn%d" % i) for i in range(2)]
        hT = sb("hT", [128, 8, TILE], BF16); hTB = Buf("hT")
        acT = sb("acT", [128, 8, TILE], BF16); acTB = Buf("acT")
        QTa = sb("QTa", [128, 4, TILE], BF16); QTb = sb("QTb", [128, 4, TILE], BF16); QTB = Buf("QT")
        KTW = PAST + 128
        KT = sb("KT", [128, 4, KTW], BF16); KTB = [Buf("KT%d" % i) for i in range(17)]
        KTS = sb("KTS", [128, 4, 128], BF16); KTSB = Buf("KTS")
        VA = sb("VA", [128, 17, 4, 130], BF16); VAB = [Buf("VA%d" % i) for i in range(17)]
        gBf = sb("gBf", [128, D], F32)
        clng = sb("clng", [128, 512], F32); clnb = sb("clnb", [128, 512], F32)
        constB = Buf("const")
        BiasN = sb("BiasN", [128, 4, 2, 128], F32); BiasB = Buf("BiasN")
        uT = sb("uTb", [128, 4, 30 + TILE], BF16); uTB = Buf("uTb")
        utail = sb("utail", [128, 4, 128], F32); utailB = Buf("utail")
        convo = sb("convo", [128, 4, TILE], F32); convoB = Buf("convo")
        NWK = 6
        WK = [sb("wk%d" % i, [128, 512], F32) for i in range(NWK)]
        WKB = [Buf("wk%d" % i) for i in range(NWK)]
        AO = [sb("ao%d" % i, [128, 512], F32) for i in range(2)]
        AOB = [Buf("ao%d" % i) for i in range(2)]
        PT = [sb("pt%d" % i, [128, 4, 128], BF16) for i in range(4)]
        PTBf = [Buf("pt%d" % i) for i in range(4)]
        AN = [sb("an%d" % i, [128, 512], BF16) for i in range(2)]
        ANB = [Buf("an%d" % i) for i in range(2)]
        CN = [sb("cn%d" % i, [128, 512], BF16) for i in range(2)]
        CNB = [Buf("cn%d" % i) for i in range(2)]
        CTs = [sb("ct%d" % i, [128, 512], F32) for i in range(2)]
        CTBs = [Buf("ct%d" % i) for i in range(2)]
        JKs = [sb("jk%d" % i, [128, 512], F32) for i in range(2)]
        JKBs = [Buf("jk%d" % i) for i in range(2)]
        CT, CTB, JK, JKB = CTs[0], CTBs[0], JKs[0], JKBs[0]
        BGST = [sb("bgst%d" % i, [128, 8], F32) for i in range(6)]
        BGSTB = [Buf("bgst%d" % i) for i in range(6)]
        NST = 12
        STT = [sb("st%d" % i, [128, 8], F32) for i in range(NST)]
        STB = [Buf("st%d" % i) for i in range(NST)]
        ident = sb("identb", [128, 128], BF16); identf = sb("identf", [128, 128], F32)
        gT = sb("gT", [128, 8, 4], F32)
        gac = sb("gac", [128, 8], F32)
        wdwT = sb("wdwT", [128, 4, 32], F32)
        cfar = sb("cfar", [128, 4], F32)
        lamt = sb("lamt", [128, 8], F32)
        epsT = sb("epsT", [128, 1], F32)
        oneT = sb("oneT", [128, 1], F32)
        ohs = WK[3]
        lamb = WK[4]
        rbs = WK[5]
        PB = [ps("pb%d" % i, [128, 512], F32) for i in range(4)]
        PBB = [Buf("pb%d" % i) for i in range(4)]
        PTR = [ps("ptr%d" % i, [128, 8, 128], BF16) for i in range(1)]
        PTRB = [Buf("ptr%d" % i) for i in range(1)]
        BGB = ps("bgb", [128, 512], F32); BGBB = Buf("bgb")
        PO = [ps("po%d" % i, [128, 512], F32) for i in range(2)]
        POB = [Buf("po%d" % i) for i in range(2)]
        gscrB = Buf("gscr")

        rr = {"pb": 0, "wk": 0, "pt": 0, "st": 0, "x": 0, "ptr": 0, "po": 0, "acn": 0}

        def nxt(kind, arr, arrB):
            i = rr[kind] % len(arr)
            rr[kind] += 1
            return arr[i], arrB[i]

        bank = lambda: nxt("pb", PB, PBB)
        wk = lambda: nxt("wk", WK, WKB)
        ptile = lambda: nxt("pt", PT, PTBf)
        stat = lambda: nxt("st", STT, STB)
        xbuf = lambda: nxt("x", X, XB)
        ptr = lambda: nxt("ptr", PTR, PTRB)
        pobank = lambda: nxt("po", PO, POB)
        bgstat = lambda: nxt("acn", BGST, BGSTB)

        def bcast_last(t, pstride, off, mid, n):
            return bass.AP(t, off, [[pstride, 128], [1, mid], [0, n]])

        T.dma("sp", identf[:], ident_in[:], writes=[constB])
        T.dma("sp", ohs[:], oh_in[:], writes=[WKB[3]])
        T.dma("sp", rbs[:, 0:128], rbp[:], writes=[WKB[5]])
        T.dma("sp", WK[0][:, :], vpa[:, 0:512], writes=[WKB[0]])
        T.dma("sp", WK[1][:, :], vpa[:, 512:1024], writes=[WKB[1]])
        T.dma("sp", WK[2][:, :], vpb[:, :], writes=[WKB[2]])
        T.dma("sp", gBf[:], bass.AP(vrow.tensor, 0, [[0, 128], [1, D]]), writes=[constB])
        T.dma("sp", clng[:], bass.AP(vrow.tensor, 1024, [[0, 128], [1, 512]]), writes=[constB])
        T.dma("sp", clnb[:], bass.AP(vrow.tensor, 1536, [[0, 128], [1, 512]]), writes=[constB])
        T.dma("sp", lamb[:, 0:256], bass.AP(vrow.tensor, 2048, [[0, 128], [1, 256]]), writes=[WKB[4]])
        T.dma("sp", cfar[:], bass.AP(rbp.tensor, 15 * 128, [[0, 128], [1, 4]]), writes=[constB])
        pool(lambda: nc.gpsimd.memset(epsT[:], EPS), writes=[constB])
        pool(lambda: nc.gpsimd.memset(oneT[:], 1.0), writes=[constB])
        dve(lambda: nc.vector.tensor_copy(ident[:], identf[:]), reads=[constB], writes=[constB])
        def late_memsets():
            pool(lambda: nc.gpsimd.memset(QTa[:], 0.0), writes=[QTB])
            pool(lambda: nc.gpsimd.memset(QTb[:], 0.0), writes=[QTB])
            pool(lambda: nc.gpsimd.memset(KT[:, :, PAST:KTW], 0.0), writes=[KTB[16]])
            pool(lambda: nc.gpsimd.memset(VA[:, 0:16, :, 128:130], 1.0), writes=VAB[0:16])
            pool(lambda: nc.gpsimd.memset(VA[:, 16, :, :], 0.0), writes=[VAB[16]])
            pool(lambda: nc.gpsimd.memset(VA[0:32, 16, :, 128:130], 1.0), writes=[VAB[16]])
            for i in range(4):
                pool(lambda i=i: nc.gpsimd.memset(PT[i][:], 0.0), writes=[PTBf[i]])

        pool(lambda: nc.gpsimd.memset(gac[:], 1.0), writes=[constB])

        for half in range(2):
            pb, pbB = bank()
            transposes([(pb[:, j * 128:(j + 1) * 128], WK[half][:, j * 128:(j + 1) * 128], identf[:]) for j in range(4)],
                       reads=[WKB[half], constB], outB=pbB)
            act(lambda pb=pb, half=half: nc.scalar.activation(
                out=gT[:, half * 4:(half + 1) * 4, 0:3],
                in_=pb[:, :].rearrange("p (j c) -> p j c", j=4)[:, :, 0:3],
                func=AF.Copy), reads=[pbB], writes=[constB])
        pb, pbB = bank()
        transposes([(pb[:, j * 128:(j + 1) * 128], WK[2][:, j * 128:(j + 1) * 128], identf[:]) for j in range(4)],
                   reads=[WKB[2], constB], outB=pbB)
        act(lambda pb=pb: nc.scalar.activation(out=wdwT[:], in_=pb[:, :].rearrange("p (j c) -> p j c", j=4)[:, :, 0:32],
                                               func=AF.Copy), reads=[pbB], writes=[constB])
        for h in range(4):
            dve(lambda h=h: nc.vector.tensor_scalar(out=gac[:, h:h + 1], in0=gT[:, 0, 2:3], scalar1=1.0 - LAM_INIT,
                                                    scalar2=None, op0=ALU.mult), reads=[constB], writes=[constB])
        dve(lambda: nc.vector.tensor_tensor(out=lamb[:, 0:64], in0=lamb[:, 0:64], in1=lamb[:, 64:128], op=ALU.mult),
            reads=[WKB[4]], writes=[WKB[4]])
        dve(lambda: nc.vector.tensor_tensor(out=lamb[:, 128:192], in0=lamb[:, 128:192], in1=lamb[:, 192:256], op=ALU.mult),
            reads=[WKB[4]], writes=[WKB[4]])
        dve(lambda: nc.vector.reduce_sum(out=lamt[:, 2:3], in_=lamb[:, 0:64], axis=AX.X), reads=[WKB[4]], writes=[constB])
        dve(lambda: nc.vector.reduce_sum(out=lamt[:, 3:4], in_=lamb[:, 128:192], axis=AX.X), reads=[WKB[4]], writes=[constB])
        act(lambda: nc.scalar.activation(out=lamt[:, 4:6], in_=lamt[:, 2:4], func=AF.Exp), reads=[constB], writes=[constB])
        dve(lambda: nc.vector.tensor_tensor(out=lamt[:, 6:7], in0=lamt[:, 4:5], in1=lamt[:, 5:6], op=ALU.subtract),
            reads=[constB], writes=[constB])
        dve(lambda: nc.vector.tensor_scalar(out=lamt[:, 1:2], in0=lamt[:, 6:7], scalar1=LAM_INIT, scalar2=-1.0,
                                            op0=ALU.add, op1=ALU.mult), reads=[constB], writes=[constB])
        bias_late = []

        def build_bias_tables():
            pb, pbB = bank()
            mm(pb[:, :], pbB, [(rbs[:, 0:128], ohs[:, :])], reads=[WKB[5], WKB[3]])
            act(lambda: nc.scalar.activation(out=WK[0][0:4, :], in_=pb[0:4, :], func=AF.Copy), reads=[pbB], writes=[WKB[0]])
            T.dma("sp", gscr[:, :], WK[0][0:4, :], reads=[WKB[0]], writes=[gscrB])
            pool(lambda: nc.gpsimd.memset(BiasN[:], 0.0), writes=[BiasB])
            pool(lambda: nc.gpsimd.memset(BiasN[64:128, :, 1, 0:64], NEG), writes=[BiasB])
            for c in range(2):
                hkc, hkcB = (CT, CTB) if c == 0 else (JK, JKB)
                for h in range(4):
                    T.dma("sp", hkc[:, h * 128:(h + 1) * 128], bass.AP(gscr.tensor, h * 512 + c * 256, [[1, 128], [1, 128]]),
                          reads=[gscrB], writes=[hkcB])
                bias_late.append(lambda c=c, hkc=hkc, hkcB=hkcB: dve(lambda: nc.vector.tensor_tensor(
                    out=BiasN[:, :, c, :], in0=BiasN[:, :, c, :],
                    in1=bass.AP(hkc, 127, [[512, 128], [128, 4], [-1, 128]]), op=ALU.add),
                    reads=[hkcB, BiasB], writes=[BiasB]))

        def chunk_src(cid):
            if cid < 5:
                return w_in.rearrange("(kc p) n -> p kc n", p=128)[:, :, cid * 512:(cid + 1) * 512]
            if cid < 7:
                return w_out.rearrange("(kc p) n -> p kc n", p=128)[:, :, (cid - 5) * 512:(cid - 4) * 512]
            if cid < 15:
                return w_up.rearrange("(kc p) n -> p kc n", p=128)[:, :, (cid - 7) * 512:(cid - 6) * 512]
            fh, g = divmod(cid - 15, 4)
            return w_down.rearrange("(kc p) n -> p kc n", p=128)[:, g * 8:(g + 1) * 8, fh * 512:(fh + 1) * 512]

        wstate = {"issued": 0}
        total_chunks = [0]

        def issue_chunk(gidx):
            tile_i, cid = divmod(gidx, NCH)
            slot = gidx % 4
            if tile_i == 0:
                T.dma("pool", ring[slot][:], chunk_src(cid), writes=[ringB[slot]])
                T.dma("sp", wscr[cid].rearrange("p (k n) -> p k n", k=8), ring[slot][:], reads=[ringB[slot]],
                      writes=[scrB], sem_buf=scrB)
            else:
                T.dma("sp", ring[slot][:], wscr[cid].rearrange("p (k n) -> p k n", k=8), reads=[scrB],
                      writes=[ringB[slot]])

        def chunk(gidx):
            while wstate["issued"] < min(gidx + 3, total_chunks[0]):
                issue_chunk(wstate["issued"])
                wstate["issued"] += 1
            return ring[gidx % 4], ringB[gidx % 4]

        def rstd_from_ss(stt, stB, c0, n, inv_n):
            act(lambda: nc.scalar.activation(out=stt[:, c0 + n:c0 + 2 * n], in_=stt[:, c0:c0 + n], func=AF.Ln,
                                             scale=inv_n, bias=epsT[:, 0:1]), reads=[stB, constB], writes=[stB])
            act(lambda: nc.scalar.activation(out=stt[:, c0 + n:c0 + 2 * n], in_=stt[:, c0 + n:c0 + 2 * n], func=AF.Exp,
                                             scale=-0.5), reads=[stB], writes=[stB])

        def norm_T(xt, NT, gsel, dst, dstB):
            stt, stB = stat()
            pool(lambda: nc.gpsimd.memset(stt[:], 0.0), writes=[stB])
            for t in range(NT):
                x, xB = xt[t]
                xn, xnB = XN[t % 2], XNB[t % 2]
                act(lambda x=x, xn=xn, t=t: nc.scalar.activation(out=xn[:], in_=x[:], func=AF.Square,
                                                                 accum_out=stt[:, t:t + 1]),
                    reads=[xB], writes=[xnB, stB])
                act(lambda t=t: nc.scalar.activation(out=stt[:, 4 + t:5 + t], in_=stt[:, t:t + 1], func=AF.Ln,
                                                     scale=1.0 / D, bias=epsT[:, 0:1]), reads=[stB, constB], writes=[stB])
                act(lambda t=t: nc.scalar.activation(out=stt[:, 4 + t:5 + t], in_=stt[:, 4 + t:5 + t], func=AF.Exp,
                                                     scale=-0.5), reads=[stB], writes=[stB])
                dve(lambda x=x, xn=xn, t=t: nc.vector.tensor_scalar(out=xn[:], in0=x[:], scalar1=stt[:, 4 + t:5 + t],
                                                                    scalar2=None, op0=ALU.mult),
                    reads=[xB, stB], writes=[xnB])
                pt_, ptB_ = ptr()
                transposes([(pt_[:, kc, :], xn[:, kc * 128:(kc + 1) * 128], ident[:]) for kc in range(8)],
                           reads=[xnB, constB], outB=ptB_)
                dve(lambda pt_=pt_, t=t: nc.vector.tensor_tensor(
                    out=dst[:, :, t * 128:(t + 1) * 128], in0=pt_[:, :, :],
                    in1=bass.AP(gT, gsel, [[32, 128], [4, 8], [0, 128]]), op=ALU.mult),
                    reads=[ptB_, constB], writes=[dstB])

        outq = {"n": 0}
        xpre = {}

        def preload_x(gt_, xin_, NT_, nmax=None):
            lst = xpre.setdefault(gt_, [])
            while len(lst) < (NT_ if nmax is None else min(NT_, nmax)):
                t_ = len(lst)
                x, xB = xbuf()
                T.dma("sp", x[:], xin_[t_ * 128:(t_ + 1) * 128, :], writes=[xB])
                lst.append((x, xB))
            return lst
        dbgB = Buf("dbg")

        def dump(idx, ap, n, B):
            if debug:
                T.dma("pool", dbg[idx, :, 0:n], ap, reads=[B], is_output=True)

        def process_tile(gt, kind, xin, yout, kout, vout, ntok, seq_tile, cp_out, next_x=None):
            NT = ntok // 128
            cbase = gt * NCH
            xt = preload_x(gt, xin, NT)
            xpre.pop(gt, None)
            norm_T(xt, NT, 0, hT, hTB)
            def dg_build(cb):
                dve(lambda: nc.vector.tensor_tensor(
                    out=bass.AP(hff, cb * 3968, [[16384, 128], [128, 31], [1, 128]]),
                    in0=bass.AP(ident, 0, [[128, 128], [0, 31], [1, 128]]),
                    in1=bass.AP(wdwT, cb * 32, [[128, 128], [1, 31], [0, 128]]), op=ALU.mult),
                    reads=[constB], writes=hffB)

            for cb_ in range(4):
                dg_build(cb_)
            wq, wqB = chunk(cbase + 0)
            for h in range(4):
                pb, pbB = bank()
                mm(pb[:, :ntok], pbB, [(wq[:, kc, h * 128:(h + 1) * 128], hT[:, kc, :ntok]) for kc in range(8)],
                   reads=[wqB, hTB])
                act(lambda pb=pb, h=h: nc.scalar.activation(out=QTa[0:64, h, :ntok], in_=pb[0:64, :ntok], func=AF.Copy,
                                                            scale=0.125), reads=[pbB], writes=[QTB])
                act(lambda pb=pb, h=h: nc.scalar.activation(out=QTb[64:128, h, :ntok], in_=pb[64:128, :ntok],
                                                            func=AF.Copy, scale=0.125), reads=[pbB], writes=[QTB])
            wkc, wkB_ = chunk(cbase + 1)
            if kind == 'p':
                kt0 = seq_tile * 4
                ktb = KTB[kt0:kt0 + 4]
            for h in range(4):
                pb, pbB = bank()
                mm(pb[:, :ntok], pbB, [(wkc[:, kc, h * 128:(h + 1) * 128], hT[:, kc, :ntok]) for kc in range(8)],
                   reads=[wkB_, hTB])
                if kind == 'p':
                    act(lambda pb=pb, h=h: nc.scalar.activation(out=KT[:, h, kt0 * 128:kt0 * 128 + ntok],
                                                                in_=pb[:, :ntok], func=AF.Copy),
                        reads=[pbB], writes=ktb)
                else:
                    act(lambda pb=pb, h=h: nc.scalar.activation(out=KTS[:, h, :], in_=pb[:, :ntok], func=AF.Copy),
                        reads=[pbB], writes=[KTSB])
            for t in range(NT):
                pb, pbB = bank()
                mm(pb[:, :], pbB, [(hT[:, kc, t * 128:(t + 1) * 128], wkc[:, kc, :]) for kc in range(8)],
                   reads=[wkB_, hTB])
                w, wB = wk()
                act(lambda pb=pb, w=w: nc.scalar.activation(out=w[:], in_=pb[:], func=AF.Copy), reads=[pbB], writes=[wB])
                T.dma("sp", kout[t * 128:(t + 1) * 128, :], w[:], reads=[wB], is_output=True)
            wv, wvB = chunk(cbase + 2)
            vstage = None
            for t in range(NT):
                pb, pbB = bank()
                mm(pb[:, :], pbB, [(hT[:, kc, t * 128:(t + 1) * 128], wv[:, kc, :]) for kc in range(8)],
                   reads=[wvB, hTB])
                w, wB = wk() if kind == 'p' else (AO[1], AOB[1])
                act(lambda pb=pb, w=w: nc.scalar.activation(out=w[:], in_=pb[:], func=AF.Copy), reads=[pbB], writes=[wB])
                T.dma("sp", vout[t * 128:(t + 1) * 128, :], w[:], reads=[wB], is_output=True)
                if kind == 'p':
                    vt = seq_tile * 4 + t
                    pool(lambda w=w, vt=vt: nc.gpsimd.tensor_copy(out=VA[:, vt, :, 0:128],
                                                                  in_=w[:, :].rearrange("p (h d) -> p h d", h=4)),
                         reads=[wB], writes=[VAB[vt]])
                else:
                    vstage = (w, wB)
            wval, wvalB = chunk(cbase + 3)
            wgate, wgateB = chunk(cbase + 4)
            for cb in range(4):
                pa, paB = bank()
                mm(pa[:, :ntok], paB, [(wval[:, kc, cb * 128:(cb + 1) * 128], hT[:, kc, :ntok]) for kc in range(8)],
                   reads=[wvalB, hTB])
                pg, pgB = bank()
                mm(pg[:, :ntok], pgB, [(wgate[:, kc, cb * 128:(cb + 1) * 128], hT[:, kc, :ntok]) for kc in range(8)],
                   reads=[wgateB, hTB])
                sg, sgB = wk()
                act(lambda pg=pg, sg=sg: nc.scalar.activation(out=sg[:, :ntok], in_=pg[:, :ntok], func=AF.Sigmoid),
                    reads=[pgB], writes=[sgB])
                if kind == 's' or cp_out is not None:
                    dve(lambda pa=pa, sg=sg, cb=cb: nc.vector.tensor_tensor(out=utail[:, cb, :], in0=pa[:, ntok - 128:ntok],
                                                                            in1=sg[:, ntok - 128:ntok], op=ALU.mult),
                        reads=[paB, sgB], writes=[utailB])
                if kind == 'p':
                    dve(lambda pa=pa, sg=sg, cb=cb: nc.vector.tensor_tensor(out=uT[:, cb, 30:30 + ntok], in0=pa[:, :ntok],
                                                                            in1=sg[:, :ntok], op=ALU.mult),
                        reads=[paB, sgB], writes=[uTB])
                else:
                    dve(lambda pa=pa, sg=sg, cb=cb: nc.vector.tensor_tensor(
                        out=uT[:, cb, 0:248].rearrange("p (s j) -> p s j", s=4)[:, :, 30:62],
                        in0=pa[:, 0:128].rearrange("p (s j) -> p s j", s=4),
                        in1=sg[:, 0:128].rearrange("p (s j) -> p s j", s=4), op=ALU.mult),
                        reads=[paB, sgB], writes=[uTB])
            if next_x is not None:
                preload_x(gt + 1, next_x[0], next_x[1], nmax=NX - 4)
            def dg(cb, j):
                return bass.AP(hff, cb * 3968 + j * 128, [[16384, 128], [1, 128]])

            def conv_mm(cb, j0, j1):
                if kind == 'p':
                    for j in range(j0, j1):
                        T.op("pe", lambda j=j: nc.tensor.matmul(BGB[:, 0:ntok], dg(cb, j), uT[:, cb, j:j + ntok],
                                                                start=(j == 0), stop=(j == 30)),
                             reads=hffB + [uTB], writes=[BGBB], inc=(j == j1 - 1))
                else:
                    for s_ in range(NSTR):
                        for j in range(j0, j1):
                            T.op("pe", lambda j=j, s_=s_: nc.tensor.matmul(
                                BGB[:, s_ * 32:(s_ + 1) * 32], dg(cb, j), uT[:, cb, s_ * 62 + j:s_ * 62 + j + 32],
                                start=(j == 0), stop=(j == 30)),
                                 reads=hffB + [uTB], writes=[BGBB], inc=(j == j1 - 1 and s_ == NSTR - 1))

            def conv_evac(cb):
                dve(lambda: nc.vector.tensor_scalar(out=convo[:, cb, :ntok], in0=BGB[:, 0:ntok], scalar1=wdwT[:, cb, 31:32],
                                                    scalar2=None, op0=ALU.add), reads=[BGBB, constB], writes=[convoB])

            def conv_state_out():
                if kind == 'p':
                    if cp_out is not None:
                        transposes([(BGB[:, cb * 128:(cb + 1) * 128], utail[:, cb, :], identf[:]) for cb in range(4)],
                                   reads=[utailB, constB], outB=BGBB)
                        act(lambda: nc.scalar.activation(out=CT[:], in_=BGB[:], func=AF.Copy), reads=[BGBB], writes=[CTB])
                        T.dma("sp", cp_out, CT[98:128, :], reads=[CTB], is_output=True)
                    else:
                        pool(lambda: nc.gpsimd.tensor_copy(out=uT[:, :, 0:30], in_=uT[:, :, ntok:ntok + 30]), reads=[uTB],
                             writes=[uTB])
                else:
                    transposes([(BGB[:, cb * 128:(cb + 1) * 128], utail[:, cb, :], identf[:]) for cb in range(4)],
                               reads=[utailB, constB], outB=BGBB)
                    act(lambda: nc.scalar.activation(out=CT[:], in_=BGB[:], func=AF.Copy), reads=[BGBB], writes=[CTB])
                    for s_ in range(NSTR):
                        T.dma("sp", cso[s_], CT[32 * s_ + 2:32 * s_ + 32, :], reads=[CTB], is_output=True)

            P1, P2 = [], []
            if kind == 's':
                pass
            for cb in range(4):
                if kind == 'p':
                    for j0 in range(0, 31, 4):
                        P2.append(lambda cb=cb, j0=j0: conv_mm(cb, j0, min(j0 + 4, 31)))
                else:
                    P2.append(lambda cb=cb: conv_mm(cb, 0, 31))
                P2.append(lambda cb=cb: conv_evac(cb))
            P2.append(conv_state_out)

            def convln_tasks(t):
                cn, cnB = CN[t % 2], CNB[t % 2]
                stt, stB = BGST[t], BGSTB[t]
                CT, CTB, JK, JKB = CTs[t % 2], CTBs[t % 2], JKs[t % 2], JKBs[t % 2]

                def c1a():
                    transposes([(BGB[:, cb * 128:(cb + 1) * 128], convo[:, cb, t * 128:(t + 1) * 128], identf[:])
                                for cb in range(4)], reads=[convoB, constB], outB=BGBB)
                    pool(lambda: nc.gpsimd.memset(stt[:], 0.0), writes=[stB])
                    act(lambda: nc.scalar.activation(out=CT[:], in_=BGB[:], func=AF.Copy, accum_out=stt[:, 0:1]),
                        reads=[BGBB], writes=[CTB, stB])

                def c1b():
                    act(lambda: nc.scalar.activation(out=cn[:], in_=BGB[:], func=AF.Square, accum_out=stt[:, 1:2]),
                        reads=[BGBB], writes=[cnB, stB])

                def c2():
                    dve(lambda: nc.vector.tensor_scalar(out=stt[:, 2:3], in0=stt[:, 0:1], scalar1=1.0 / 512, scalar2=None,
                                                        op0=ALU.mult), reads=[stB], writes=[stB])
                    dve(lambda: nc.vector.tensor_tensor(out=stt[:, 3:4], in0=stt[:, 2:3], in1=stt[:, 2:3], op=ALU.mult),
                        reads=[stB], writes=[stB])
                    dve(lambda: nc.vector.scalar_tensor_tensor(out=stt[:, 4:5], in0=stt[:, 1:2], scalar=1.0 / 512,
                                                               in1=stt[:, 3:4], op0=ALU.mult, op1=ALU.subtract),
                        reads=[stB], writes=[stB])

                def c3():
                    rstd_from_ss(stt, stB, 4, 1, 1.0)

                def c4a():
                    dve(lambda: nc.vector.scalar_tensor_tensor(out=stt[:, 6:7], in0=stt[:, 2:3], scalar=-1.0, in1=stt[:, 5:6],
                                                               op0=ALU.mult, op1=ALU.mult), reads=[stB], writes=[stB])
                    dve(lambda: nc.vector.tensor_scalar(out=CT[:], in0=CT[:], scalar1=stt[:, 5:6], scalar2=stt[:, 6:7],
                                                        op0=ALU.mult, op1=ALU.add), reads=[CTB, stB], writes=[CTB])

                def c5a():
                    dve(lambda: nc.vector.tensor_tensor(out=CT[:], in0=CT[:], in1=clng[:], op=ALU.mult),
                        reads=[CTB, constB], writes=[CTB])

                def c5b():
                    dve(lambda: nc.vector.tensor_tensor(out=CT[:], in0=CT[:], in1=clnb[:], op=ALU.add),
                        reads=[CTB, constB], writes=[CTB])

                def c6a():
                    act(lambda: nc.scalar.activation(out=JK[:], in_=CT[:], func=AF.Exp, scale=-1.0), reads=[CTB], writes=[JKB])

                def c6b():
                    act(lambda: nc.scalar.activation(out=JK[:], in_=JK[:], func=AF.Ln, bias=oneT[:, 0:1]), reads=[JKB, constB],
                        writes=[JKB])

                def c6c():
                    act(lambda: nc.scalar.activation(out=JK[:], in_=JK[:], func=AF.Exp, scale=-1.0), reads=[JKB], writes=[JKB])

                def c7():
                    dve(lambda: nc.vector.tensor_tensor(out=cn[:], in0=CT[:], in1=JK[:], op=ALU.mult),
                        reads=[CTB, JKB], writes=[cnB])

                def c8():
                    pt_, ptB_ = ptr()
                    transposes([(pt_[:, kc, :], cn[:, kc * 128:(kc + 1) * 128], ident[:]) for kc in range(4)],
                               reads=[cnB, constB], outB=ptB_)
                    dve(lambda: nc.vector.tensor_copy(out=acT[:, 4:8, t * 128:(t + 1) * 128], in_=pt_[:, 0:4, :]),
                        reads=[ptB_], writes=[acTB])
                return [c1a, c1b, c2, c3, c4a, c5a, c5b, c6a, c6b, c6c, c7], c8

            def subln_tasks(t, ao, aoB):
                an, anB = AN[t % 2], ANB[t % 2]
                stt, stB = BGST[4 + t % 2], BGSTB[4 + t % 2]

                def s1(h):
                    if h == 0:
                        pool(lambda: nc.gpsimd.memset(stt[:], 0.0), writes=[stB])
                    act(lambda h=h: nc.scalar.activation(out=an[:, h * 128:(h + 1) * 128],
                                                         in_=ao[:, h * 128:(h + 1) * 128], func=AF.Square,
                                                         accum_out=stt[:, h:h + 1]), reads=[aoB], writes=[anB, stB])

                def s2():
                    rstd_from_ss(stt, stB, 0, 4, 1.0 / 128)

                def s3():
                    dve(lambda: nc.vector.tensor_tensor(out=an[:, :].rearrange("p (h d) -> p h d", h=4),
                                                        in0=ao[:, :].rearrange("p (h d) -> p h d", h=4),
                                                        in1=bass.AP(stt, 4, [[8, 128], [1, 4], [0, 128]]), op=ALU.mult),
                        reads=[aoB, stB], writes=[anB])

                def s4():
                    pt_, ptB_ = ptr()
                    transposes([(pt_[:, kc, :], an[:, kc * 128:(kc + 1) * 128], ident[:]) for kc in range(4)],
                               reads=[anB, constB], outB=ptB_)
                    dve(lambda: nc.vector.tensor_tensor(out=acT[:, 0:4, t * 128:(t + 1) * 128], in0=pt_[:, 0:4, :],
                                                        in1=bass.AP(gac, 0, [[8, 128], [1, 4], [0, 128]]), op=ALU.mult),
                        reads=[ptB_, constB], writes=[acTB])
                return [lambda: s1(0), lambda: s1(1), lambda: s1(2), lambda: s1(3), s2, s3, s4]

            def run_tasks(q, k):
                for _ in range(min(k, len(q))):
                    q.pop(0)()

            while bias_late:
                bias_late.pop(0)()

            def attn_items(gb):
                items = []
                for h in range(4):
                    for m in range(2):
                        far = list(range(0, max(gb - 1, 0)))
                        near = ([gb - 1] if gb >= 1 else []) + [gb]
                        groups = [('near', near)] + [('far', far[i:i + 4]) for i in range(0, len(far), 4)]
                        for gi, (kd, tl) in enumerate(groups):
                            items.append(dict(h=h, m=m, kind=kd, tiles=tl, first=(gi == 0), last=(gi == len(groups) - 1)))
                return items

            def attn_multi(blocks, weave=None):
                items = []
                for bi, blk in enumerate(blocks):
                    its = attn_items(blk['gb'])
                    for k_, it in enumerate(its):
                        it['blk'] = bi
                        it['left'] = len(its) - k_
                        it['first_in_blk'] = (k_ == 0)
                    items.extend(its)
                n = len(items)
                LAG = 3
                state = {}
                deferred = []

                def emit_qk(it):
                    blk = blocks[it['blk']]
                    qc0, nq, qoff = blk['qc0'], blk['nq'], blk['qoff']
                    h, m = it['h'], it['m']
                    S, SB = bank()
                    it['S'], it['SB'] = S, SB
                    QTm = QTa if m == 0 else QTb
                    nt_ = len(it['tiles'])
                    for idx, a in enumerate(it['tiles']):
                        T.op("pe", lambda idx=idx, a=a: nc.tensor.matmul(
                            S[:, idx * 128 + qoff: idx * 128 + qoff + nq], KT[:, h, a * 128:(a + 1) * 128],
                            QTm[:, h, qc0:qc0 + nq], start=True, stop=True),
                             reads=[KTB[a], QTB], writes=[SB], inc=(idx == nt_ - 1))

                def emit_mid(it):
                    blk = blocks[it['blk']]
                    nq, qoff = blk['nq'], blk['qoff']
                    h, m = it['h'], it['m']
                    S, SB = it['S'], it['SB']
                    nt_ = len(it['tiles'])
                    P, PBf = ptile()
                    it['P'], it['PBf'] = P, PBf
                    Sv = S[:, 0:nt_ * 128].rearrange("p (g q) -> p g q", g=nt_)[:, :, qoff:qoff + nq]
                    if it['kind'] == 'far':
                        act(lambda: nc.scalar.activation(out=P[:, 0:nt_, qoff:qoff + nq], in_=Sv, func=AF.Exp,
                                                         bias=cfar[:, h:h + 1]), reads=[SB, constB], writes=[PBf])
                    else:
                        sn, snB, snv = it['sn']
                        act(lambda: nc.scalar.activation(out=P[:, 0:nt_, qoff:qoff + nq], in_=snv, func=AF.Exp),
                            reads=[snB], writes=[PBf])

                def emit_add(it):
                    blk = blocks[it['blk']]
                    nq, qoff = blk['nq'], blk['qoff']
                    h = it['h']
                    S, SB = it['S'], it['SB']
                    nt_ = len(it['tiles'])
                    Sv = S[:, 0:nt_ * 128].rearrange("p (g q) -> p g q", g=nt_)[:, :, qoff:qoff + nq]
                    sn, snB = wk()
                    snv = sn[:, 0:nt_ * 128].rearrange("p (g q) -> p g q", g=nt_)[:, :, 0:nq]
                    c0 = 2 - nt_
                    dve(lambda: nc.vector.tensor_tensor(out=snv, in0=Sv, in1=BiasN[:, h, c0:2, 0:nq], op=ALU.add),
                        reads=[SB, BiasB], writes=[snB])
                    it['sn'] = (sn, snB, snv)

                def emit_pv(it):
                    blk = blocks[it['blk']]
                    nq, qoff, ao, aoB = blk['nq'], blk['qoff'], blk['ao'], blk['aoB']
                    h, m = it['h'], it['m']
                    nt_ = len(it['tiles'])
                    P, PBf = it['P'], it['PBf']
                    if it['first'] and m == 0:
                        state['po'] = pobank()
                    O, OB = state['po']
                    for idx, a in enumerate(it['tiles']):
                        T.op("pe", lambda idx=idx, a=a: nc.tensor.matmul(
                            O[:, m * 256:m * 256 + 129], P[:, idx, :], VA[:, a, h, 0:129],
                            start=(it['first'] and idx == 0), stop=(it['last'] and idx == nt_ - 1)),
                             reads=[PBf, VAB[a]], writes=[OB], inc=(idx == nt_ - 1))
                    if it['last'] and m == 1:
                        r0, r1 = qoff, qoff + nq

                        def fin(O=O, OB=OB, h=h, blk=blk):
                            stt, stB = stat()
                            dve(lambda: nc.vector.reciprocal(out=stt[r0:r1, 0:2],
                                                             in_=bass.AP(O, r0 * 512 + 128, [[512, r1 - r0], [256, 2]])),
                                reads=[OB], writes=[stB])
                            t2, t2B = wk()
                            dve(lambda: nc.vector.tensor_scalar(out=t2[r0:r1, 0:128], in0=O[r0:r1, 256:384],
                                                                scalar1=stt[r0:r1, 1:2], scalar2=lamt[r0:r1, 1:2],
                                                                op0=ALU.mult, op1=ALU.mult),
                                reads=[OB, stB, constB], writes=[t2B])
                            dve(lambda: nc.vector.scalar_tensor_tensor(out=ao[r0:r1, h * 128:(h + 1) * 128],
                                                                       in0=O[r0:r1, 0:128], scalar=stt[r0:r1, 0:1],
                                                                       in1=t2[r0:r1, 0:128], op0=ALU.mult, op1=ALU.add),
                                reads=[OB, stB, t2B], writes=[aoB])
                            if h == 3 and blk.get('on_done') is not None:
                                blk['on_done']()
                        deferred.append([2, fin])

                def tick():
                    for d in list(deferred):
                        d[0] -= 1
                        if d[0] <= 0:
                            deferred.remove(d)
                            d[1]()

                for i in range(n + LAG):
                    if i < n:
                        emit_qk(items[i])
                    if 0 <= i - 1 < n and items[i - 1]['kind'] == 'near':
                        emit_add(items[i - 1])
                    jm = i - (LAG - 1)
                    if 0 <= jm < n:
                        itm = items[jm]
                        if itm['first_in_blk'] and itm['blk'] >= 2:
                            while P1 and P1[0][0] <= itm['blk'] - 2:
                                P1.pop(0)[1]()
                        emit_mid(itm)
                    j = i - LAG
                    if j >= 0:
                        it = items[j]
                        if j >= 1:
                            tick()
                            emit_pv(items[j - 1])
                        if weave is not None:
                            weave(it['left'], n - j)
                tick()
                emit_pv(items[n - 1])
                for d in list(deferred):
                    d[1]()

            if kind == 'p':
                for t0_ in range(0, NT, 2):
                    ch0, c80 = convln_tasks(t0_)
                    ch1, c81 = convln_tasks(t0_ + 1)
                    P2.extend(ch0[0:2])
                    P2.extend(ch1[0:2])
                    for f0, f1 in zip(ch0[2:], ch1[2:]):
                        P2.append(f0)
                        P2.append(f1)
                    P2.append(c80)
                    P2.append(c81)
                blocks = []
                for t in range(NT):
                    ao, aoB = AO[t % 2], AOB[t % 2]

                    def on_done(t=t, ao=ao, aoB=aoB):
                        if gt == 0: dump(t, ao[:], 512, aoB)
                        P1.extend([(t, f) for f in subln_tasks(t, ao, aoB)])
                    blocks.append(dict(qc0=t * 128, nq=128, qoff=0, gb=seq_tile * 4 + t, ao=ao, aoB=aoB, on_done=on_done))

                def weave(left_in_block, left_total):
                    for _ in range(min(len(P1), -(-len(P1) // max(left_in_block, 1)))):
                        P1.pop(0)[1]()
                    run_tasks(P2, -(-len(P2) // max(left_total, 1)))

                attn_multi(blocks, weave)
                run_tasks(P2, len(P2))
                while P1:
                    P1.pop(0)[1]()
            else:
                run_tasks(P2, len(P2))
                ao, aoB = AO[0], AOB[0]
                vw, vwB = vstage
                def load_kc(s_):
                    hb = 16 * (s_ % 2)
                    T.dma("pool", hff[:, hb:hb + 16, :], ck[s_].rearrange("(t p) n -> p t n", p=128),
                          writes=hffB[hb // 4:hb // 4 + 4])

                load_kc(0)
                for s in range(NSTR):
                    hb = 16 * (s % 2)
                    if s + 1 < NSTR:
                        load_kc(s + 1)
                    for h in range(4):
                        T.dma("pool", VA[:, 0:16, h, 0:128],
                              cv[s][:, h * 128:(h + 1) * 128].rearrange("(t p) d -> p t d", p=128), writes=VAB[0:16])
                    for a2 in range(8):
                        pt_, ptB_ = ptr()
                        blocks = []
                        for h in range(4):
                            for aa in range(2):
                                blocks.append((pt_[:, h * 2 + aa, :], hff[:, hb + a2 * 2 + aa, h * 128:(h + 1) * 128], ident[:]))
                        transposes(blocks, reads=[hffB[(hb + a2 * 2) // 4], constB], outB=ptB_)
                        act(lambda pt_=pt_, a2=a2: nc.scalar.activation(
                            out=KT[:, :, a2 * 256:(a2 + 1) * 256],
                            in_=pt_[:, :, :].rearrange("p (h a) q -> p h (a q)", h=4), func=AF.Copy),
                            reads=[ptB_], writes=KTB[a2 * 2:a2 * 2 + 2])
                    pool(lambda s=s: nc.gpsimd.tensor_copy(out=KT[:, :, PAST:PAST + 32], in_=KTS[:, :, 32 * s:32 * s + 32]),
                         reads=[KTSB], writes=[KTB[16]])
                    T.dma("pool", VA[0:32, 16, :, 0:128], vw[32 * s:32 * s + 32, :].rearrange("p (h d) -> p h d", h=4),
                          reads=[vwB], writes=[VAB[16]])
                    attn_multi([dict(qc0=32 * s, nq=32, qoff=32 * s, gb=16, ao=ao, aoB=aoB, on_done=None)])
                chain, c8_ = convln_tasks(0)
                for f in chain + [c8_] + subln_tasks(0, ao, aoB):
                    f()

            for fh in range(2):
                wo, woB = chunk(cbase + 5 + fh)
                for t in range(NT):
                    pb, pbB = bank()
                    mm(pb[:, :], pbB, [(acT[:, kc, t * 128:(t + 1) * 128], wo[:, kc, :]) for kc in range(8)],
                       reads=[woB, acTB])
                    x, xB = xt[t]
                    dve(lambda pb=pb, x=x, fh=fh: nc.vector.tensor_tensor(out=x[:, fh * 512:(fh + 1) * 512], in0=pb[:],
                                                                          in1=x[:, fh * 512:(fh + 1) * 512], op=ALU.add),
                        reads=[pbB, xB], writes=[xB])
            if gt == 0:
                for t in range(NT): dump(8 + t, xt[t][0][:], 1024, xt[t][1])
            norm_T(xt, NT, 1, hT, hTB)
            for c in range(8):
                wu, wuB = chunk(cbase + 7 + c)
                for j in range(4):
                    pb, pbB = bank()
                    mm(pb[:, :ntok], pbB, [(wu[:, kc, j * 128:(j + 1) * 128], hT[:, kc, :ntok]) for kc in range(8)],
                       reads=[wuB, hTB])
                    r, rB = wk()
                    act(lambda pb=pb, r=r: nc.scalar.activation(out=r[:, :ntok], in_=pb[:, :ntok], func=AF.Relu),
                        reads=[pbB], writes=[rB])
                    act(lambda r=r, c=c, j=j: nc.scalar.activation(out=hff[:, c * 4 + j, :ntok], in_=r[:, :ntok], func=AF.Square),
                        reads=[rB], writes=[hffB[c]])
            stt, stB = stat()
            pool(lambda: nc.gpsimd.memset(stt[:], 0.0), writes=[stB])
            for fh in range(2):
                accs = [bank() for _ in range(NT)]
                for g in range(4):
                    wd, wdB = chunk(cbase + 15 + fh * 4 + g)
                    for t in range(NT):
                        acc, accB = accs[t]
                        mm(acc[:, :], accB, [(hff[:, g * 8 + kc, t * 128:(t + 1) * 128], wd[:, kc, :]) for kc in range(8)],
                           reads=[wdB, hffB[2 * g], hffB[2 * g + 1]], start=(g == 0), stop=(g == 3))
                for t in range(NT):
                    acc, accB = accs[t]
                    x, xB = xt[t]
                    dve(lambda acc=acc, x=x, fh=fh: nc.vector.tensor_tensor(out=x[:, fh * 512:(fh + 1) * 512], in0=acc[:],
                                                                            in1=x[:, fh * 512:(fh + 1) * 512], op=ALU.add),
                        reads=[accB, xB], writes=[xB])
            if gt == 0:
                for t in range(NT): dump(12 + t, xt[t][0][:], 1024, xt[t][1])
            for t in range(NT):
                x, xB = xt[t]
                xn, xnB = XN[t % 2], XNB[t % 2]
                act(lambda x=x, xn=xn, t=t: nc.scalar.activation(out=xn[:], in_=x[:], func=AF.Square,
                                                                 accum_out=stt[:, t:t + 1]), reads=[xB], writes=[xnB, stB])
            rstd_from_ss(stt, stB, 0, 4, 1.0 / D)
            for t in range(NT):
                x, xB = xt[t]
                dve(lambda x=x, t=t: nc.vector.scalar_tensor_tensor(out=x[:], in0=x[:], scalar=stt[:, 4 + t:5 + t], in1=gBf[:],
                                                                    op0=ALU.mult, op1=ALU.mult),
                    reads=[xB, stB, constB], writes=[xB])
                T.dma("sp", yout[t * 128:(t + 1) * 128, :], x[:], reads=[xB], is_output=True)

        n_ptiles = NSEQ * (SEQ // TILE)
        total_chunks[0] = (n_ptiles + 1) * NCH
        gt = 0
        preload_x(0, xp[0:TILE, :], TILE // 128)
        chunk(0)
        late_memsets()
        build_bias_tables()
        for sq in range(NSEQ):
            pool(lambda: nc.gpsimd.memset(uT[:, :, 0:30], 0.0), writes=[uTB])
            for ti in range(SEQ // TILE):
                r0 = sq * SEQ + ti * TILE
                last = (ti == SEQ // TILE - 1)
                if gt + 1 < n_ptiles:
                    nx = (xp[r0 + TILE:r0 + 2 * TILE, :], TILE // 128)
                else:
                    nx = (xs, 1)
                process_tile(gt, 'p', xp[r0:r0 + TILE, :], yp[r0:r0 + TILE, :], kp[r0:r0 + TILE, :], vp[r0:r0 + TILE, :],
                             TILE, ti, cpo[sq] if last else None, next_x=nx)
                gt += 1
        w, wB = wk()
        T.dma("sp", w[:], scs_in[:], writes=[wB])
        pb, pbB = bank()
        transposes([(pb[:, cb * 128:(cb + 1) * 128], w[:, cb * 128:(cb + 1) * 128], identf[:]) for cb in range(4)],
                   reads=[wB, constB], outB=pbB)
        for cb in range(4):
            act(lambda cb=cb: nc.scalar.activation(
                out=uT[:, cb, 0:248].rearrange("p (s j) -> p s j", s=4)[:, :, 0:30],
                in_=pb[:, cb * 128:(cb + 1) * 128].rearrange("p (s j) -> p s j", s=4)[:, :, 0:30], func=AF.Copy),
                reads=[pbB], writes=[uTB])
        process_tile(gt, 's', xs, ys, kso, vso, 128, None, None)
        T.finish()
    return nc


_PROG = {}


def kernel(x_prompt, x_sample, cache_k, cache_v, state_conv, rel_bias, ln1_g, w_in, lambda_q1, lambda_k1,
           lambda_q2, lambda_k2, subln_g, w_dw, b_dw, conv_ln_g, conv_ln_b, w_out, ln2_g, w_up, w_down, ln_f_g):
    f = lambda a: np.ascontiguousarray(np.asarray(a, dtype=np.float32))
    n = 8
    x_prompt, x_sample, cache_k, cache_v, state_conv = map(f, (x_prompt, x_sample, cache_k, cache_v, state_conv))
    vpa = np.zeros((128, D), np.float32)
    vpa[0] = f(ln1_g)[0]; vpa[1] = f(ln2_g)[0]; vpa[2, 0:128] = f(subln_g)[0]
    vpb = np.zeros((128, 512), np.float32)
    vpb[0:31] = f(w_dw)[0]; vpb[31] = f(b_dw)[0]
    vrow = np.concatenate([f(ln_f_g).reshape(-1), f(conv_ln_g)[0], f(conv_ln_b)[0], f(lambda_q1)[0], f(lambda_k1)[0],
                           f(lambda_q2)[0], f(lambda_k2)[0]]).reshape(1, -1).astype(np.float32)
    rbp = np.zeros((128, 128), np.float32)
    rbp[0:32, 0:4] = f(rel_bias)
    ident = np.eye(128, dtype=np.float32)
    oh = _onehot_const()
    shared = dict(w_in=f(w_in)[0], w_out=f(w_out)[0], w_up=f(w_up)[0], w_down=f(w_down)[0], vpa=vpa, vpb=vpb, vrow=vrow,
                  rbp=rbp, ident=ident, oh=oh)
    in_maps = []
    for c in range(n):
        scs = np.zeros((NSTR, 32, 512), np.float32)
        scs[:, 0:30] = state_conv[0, NSTR * c:NSTR * (c + 1)]
        m = dict(shared)
        m.update(xp=x_prompt[NSEQ * c:NSEQ * (c + 1)].reshape(NSEQ * SEQ, D),
                 xs=x_sample[NSTR * c:NSTR * (c + 1)].reshape(NSTR * DECT, D),
                 ck=cache_k[0, NSTR * c:NSTR * (c + 1)].reshape(NSTR, PAST, 512),
                 cv=cache_v[0, NSTR * c:NSTR * (c + 1)].reshape(NSTR, PAST, 512),
                 scs=scs.reshape(128, 512))
        in_maps.append(m)
    if "nc" not in _PROG:
        _PROG["nc"] = build_program()
    res = run_bass_kernel_spmd(_PROG["nc"], in_maps, core_ids=list(range(n)))
    R = res.results
    cat = lambda k: np.concatenate([np.asarray(r[k], dtype=np.float32) for r in R], axis=0)
    y_prompt = cat("yp").reshape(16, SEQ, D)
    y_sample = cat("ys").reshape(32, DECT, D)
    k_prompt = cat("kp").reshape(1, 16, SEQ, 4, 128)
    v_prompt = cat("vp").reshape(1, 16, SEQ, 4, 128)
    conv_prompt = cat("cpo").reshape(1, 16, 30, 512)
    k_sample = cat("kso").reshape(1, 32, DECT, 4, 128)
    v_sample = cat("vso").reshape(1, 32, DECT, 4, 128)
    conv_sample = cat("cso").reshape(1, 32, 30, 512)
    return (y_prompt, y_sample, k_prompt, v_prompt, conv_prompt, k_sample, v_sample, conv_sample)
```
